# Optimizing a Trainium2 kernel written in Bass

```python
import jax, jax.numpy as jnp
from jax import lax
import numpy as np

D_MODEL = 1024
BATCH = 8
SEQ = 2048
DEPTH = 2

CHUNK = 64
N_MEM = 256
HEAD_DIM = 64
MIX_WIDTH = D_MODEL
N_MEM_HEADS = 4
MEM_WIDTH = N_MEM_HEADS * HEAD_DIM
TOK_WIDTH = MIX_WIDTH - MEM_WIDTH
N_FOX_HEADS = TOK_WIDTH // HEAD_DIM
GMLP_BLOCK = 128
GMLP_GROUPS = 4
GMLP_GROUP_WIDTH = TOK_WIDTH // GMLP_GROUPS
D_FF = 2816
CONV_WIDTH = 3
Q_BLOCK = 128
N_A = DEPTH // 2
N_B = DEPTH - N_A
EPS = 1e-6

kernel_name = "yoco_gmlp_fox_hybrid"


def rmsnorm(x, g):
    xf = x.astype(jnp.float32)
    y = xf * lax.rsqrt(jnp.mean(xf * xf, axis=-1, keepdims=True) + EPS)
    return (y * g.astype(jnp.float32)).astype(x.dtype)


def memory_attention(q_mem, mem, mem_norm, w_mem_kv):
    b, s = q_mem.shape[:2]
    kv = rmsnorm(mem, mem_norm) @ w_mem_kv
    k = kv[..., :MEM_WIDTH].reshape(b, N_MEM, N_MEM_HEADS, HEAD_DIM)
    v = kv[..., MEM_WIDTH:].reshape(b, N_MEM, N_MEM_HEADS, HEAD_DIM)
    q = q_mem.reshape(b, s, N_MEM_HEADS, HEAD_DIM)
    logits = jnp.einsum("bqhd,bkhd->bhqk", q, k).astype(jnp.float32) * (HEAD_DIM ** -0.5)
    p = jax.nn.softmax(logits, axis=-1).astype(v.dtype)
    o = jnp.einsum("bhqk,bkhd->bqhd", p, v)
    return o.reshape(b, s, MEM_WIDTH)


def gmlp_spatial_gating(u, v, v_norm, w_s, b_s):
    b, s, _ = v.shape
    n = s // GMLP_BLOCK
    vn = rmsnorm(v, v_norm).reshape(b, n, GMLP_BLOCK, GMLP_GROUPS, GMLP_GROUP_WIDTH)
    causal = jnp.tril(jnp.ones((GMLP_BLOCK, GMLP_BLOCK), dtype=bool))
    w = jnp.where(causal[None], w_s, jnp.zeros_like(w_s))
    mixed = jnp.einsum("gts,bnsgc->bntgc", w, vn) + b_s.T[None, None, :, :, None]
    return u * mixed.reshape(b, s, TOK_WIDTH)


def forgetting_attention(q, k, v, log_f_cum):
    s = q.shape[1]
    scale = HEAD_DIM ** -0.5
    c = jnp.transpose(log_f_cum, (0, 2, 1))
    outs = []
    for i in range(s // Q_BLOCK):
        q0, q1 = i * Q_BLOCK, (i + 1) * Q_BLOCK
        logits = jnp.einsum("bqhd,bkhd->bhqk", q[:, q0:q1], k[:, :q1]).astype(jnp.float32) * scale
        decay = c[:, :, q0:q1, None] - c[:, :, None, :q1]
        qpos = jnp.arange(q0, q1)[:, None]
        kpos = jnp.arange(q1)[None, :]
        logits = jnp.where(qpos >= kpos, logits + decay, -jnp.inf)
        p = jax.nn.softmax(logits, axis=-1).astype(v.dtype)
        outs.append(jnp.einsum("bhqk,bkhd->bqhd", p, v[:, :q1]))
    return jnp.concatenate(outs, axis=1)


def conv_ffn(x, w_in, conv_w, conv_b, w_out):
    s = x.shape[1]
    h = x @ w_in
    hp = jnp.pad(h, ((0, 0), (CONV_WIDTH - 1, 0), (0, 0)))
    hc = conv_b + conv_w[CONV_WIDTH - 1] * h
    for j in range(CONV_WIDTH - 1):
        hc = hc + conv_w[j] * hp[:, j:j + s]
    gate, up = hc[..., :D_FF], hc[..., D_FF:]
    return (jax.nn.silu(gate) * up) @ w_out


def setup_inputs(seed: int = 0) -> dict:
    key = jax.random.key(seed)
    ks = iter(jax.random.split(key, 64))

    def nrm(shape, scale):
        return jax.random.normal(next(ks), shape, jnp.float32) * scale

    def gain(shape):
        return 1.0 + nrm(shape, 0.05)

    D = D_MODEL
    inv = D ** -0.5
    inp = {}
    inp["x"] = nrm((BATCH, SEQ, D), 1.0)
    inp["mem"] = nrm((BATCH, N_MEM, D), 1.0)
    inp["a_norm1"] = gain((N_A, D))
    inp["a_w_in"] = nrm((N_A, D, 2 * TOK_WIDTH + MEM_WIDTH), inv)
    inp["a_v_norm"] = gain((N_A, TOK_WIDTH))
    inp["a_w_s"] = nrm((N_A, GMLP_GROUPS, GMLP_BLOCK, GMLP_BLOCK), 0.5 * GMLP_BLOCK ** -0.5)
    inp["a_b_s"] = 1.0 + nrm((N_A, GMLP_GROUPS, GMLP_BLOCK), 0.1)
    inp["a_mem_norm"] = gain((N_A, D))
    inp["a_w_mem_kv"] = nrm((N_A, D, 2 * MEM_WIDTH), inv)
    inp["a_w_out"] = nrm((N_A, MIX_WIDTH, D), MIX_WIDTH ** -0.5)
    inp["a_norm2"] = gain((N_A, D))
    inp["a_ffn_in"] = nrm((N_A, D, 2 * D_FF), inv)
    inp["a_ffn_conv"] = nrm((N_A, CONV_WIDTH, 2 * D_FF), CONV_WIDTH ** -0.5)
    inp["a_ffn_conv_b"] = nrm((N_A, 2 * D_FF), 0.02)
    inp["a_ffn_out"] = nrm((N_A, D_FF, D), D_FF ** -0.5)
    inp["kv_norm"] = gain((D,))
    inp["w_kv"] = nrm((D, 2 * TOK_WIDTH + N_FOX_HEADS), inv)
    inp["b_f"] = jax.random.uniform(next(ks), (N_FOX_HEADS,), jnp.float32, 1.0, 6.0)
    inp["b_norm1"] = gain((N_B, D))
    inp["b_w_q"] = nrm((N_B, D, TOK_WIDTH + MEM_WIDTH), inv)
    inp["b_mem_norm"] = gain((N_B, D))
    inp["b_w_mem_kv"] = nrm((N_B, D, 2 * MEM_WIDTH), inv)
    inp["b_w_out"] = nrm((N_B, MIX_WIDTH, D), MIX_WIDTH ** -0.5)
    inp["b_norm2"] = gain((N_B, D))
    inp["b_ffn_in"] = nrm((N_B, D, 2 * D_FF), inv)
    inp["b_ffn_conv"] = nrm((N_B, CONV_WIDTH, 2 * D_FF), CONV_WIDTH ** -0.5)
    inp["b_ffn_conv_b"] = nrm((N_B, 2 * D_FF), 0.02)
    inp["b_ffn_out"] = nrm((N_B, D_FF, D), D_FF ** -0.5)
    inp["final_norm"] = gain((D,))
    return inp


def reference(x, mem,
              a_norm1, a_w_in, a_v_norm, a_w_s, a_b_s, a_mem_norm, a_w_mem_kv, a_w_out,
              a_norm2, a_ffn_in, a_ffn_conv, a_ffn_conv_b, a_ffn_out,
              kv_norm, w_kv, b_f,
              b_norm1, b_w_q, b_mem_norm, b_w_mem_kv, b_w_out,
              b_norm2, b_ffn_in, b_ffn_conv, b_ffn_conv_b, b_ffn_out,
              final_norm):
    b, s, _ = x.shape
    k_sh = v_sh = log_f_cum = None
    for layer in range(DEPTH):
        if layer < N_A:
            i = layer
            h = rmsnorm(x, a_norm1[i])
            z = h @ a_w_in[i]
            u = jax.nn.gelu(z[..., :TOK_WIDTH])
            vv = jax.nn.gelu(z[..., TOK_WIDTH:2 * TOK_WIDTH])
            q_mem = z[..., 2 * TOK_WIDTH:]
            tok = gmlp_spatial_gating(u, vv, a_v_norm[i], a_w_s[i], a_b_s[i])
            mem_o = memory_attention(q_mem, mem, a_mem_norm[i], a_w_mem_kv[i])
            x = x + jnp.concatenate([tok, mem_o], axis=-1) @ a_w_out[i]
            x = x + conv_ffn(rmsnorm(x, a_norm2[i]), a_ffn_in[i], a_ffn_conv[i], a_ffn_conv_b[i], a_ffn_out[i])
            if layer == N_A - 1:
                kvf = rmsnorm(x, kv_norm) @ w_kv
                k_sh = kvf[..., :TOK_WIDTH].reshape(b, s, N_FOX_HEADS, HEAD_DIM)
                v_sh = kvf[..., TOK_WIDTH:2 * TOK_WIDTH].reshape(b, s, N_FOX_HEADS, HEAD_DIM)
                f_logit = kvf[..., 2 * TOK_WIDTH:].astype(jnp.float32) + b_f.astype(jnp.float32)
                log_f_cum = jnp.cumsum(jax.nn.log_sigmoid(f_logit), axis=1)
        else:
            j = layer - N_A
            h = rmsnorm(x, b_norm1[j])
            z = h @ b_w_q[j]
            q = z[..., :TOK_WIDTH].reshape(b, s, N_FOX_HEADS, HEAD_DIM)
            q_mem = z[..., TOK_WIDTH:]
            tok = forgetting_attention(q, k_sh, v_sh, log_f_cum).reshape(b, s, TOK_WIDTH)
            mem_o = memory_attention(q_mem, mem, b_mem_norm[j], b_w_mem_kv[j])
            x = x + jnp.concatenate([tok, mem_o], axis=-1) @ b_w_out[j]
            x = x + conv_ffn(rmsnorm(x, b_norm2[j]), b_ffn_in[j], b_ffn_conv[j], b_ffn_conv_b[j], b_ffn_out[j])
    return rmsnorm(x, final_norm)
```

```python
import contextlib
import os
import numpy as np
import concourse.bass as bass
import concourse.mybir as mybir
from concourse.bass_utils import run_bass_kernel_spmd

F32 = mybir.dt.float32
BF16 = mybir.dt.bfloat16
AF = mybir.ActivationFunctionType
ALU = mybir.AluOpType

S = 2048
D = 1024
NT = 16
KC = 8
NMEM = 256
TOK = 768
DFF = 2816
NCH = 22
EPS = 1e-6
NCORES = 8

SB_BLK = 64
PS_BLK = 2048
PS_OFF = 1 << 14


class View:
    __slots__ = ("ap", "blocks")

    def __init__(self, ap, blocks):
        self.ap = ap
        self.blocks = blocks


class Tile:
    def __init__(self, ap, space, base, shape, esize):
        self.ap = ap
        self.space = space
        self.shape = tuple(shape)
        self.esize = esize
        n = int(np.prod(shape))
        self.offs = (base + np.arange(n, dtype=np.int64) * esize).reshape(shape)

    def __getitem__(self, idx):
        if not isinstance(idx, tuple):
            idx = (idx,)
        f = idx[1:]
        sub = self.offs[f] if f else self.offs
        sub = np.asarray(sub).ravel()
        blk = SB_BLK if self.space == "sb" else PS_BLK
        lo = sub // blk
        hi = (sub + self.esize - 1) // blk
        b = np.unique(np.concatenate([lo, hi]))
        if self.space == "ps":
            b = b + PS_OFF
        return View(self.ap[idx], b)

    def all(self):
        return self[(slice(None),)]


class Prog:
    ENGS = ("pe", "act", "dve", "pool", "sp")

    def __init__(self, nc):
        self.nc = nc
        self.ops = []
        nb = PS_OFF + 512
        self.last_w = np.full(nb, -1, dtype=np.int64)
        self.last_r = {}
        self.nb = nb
        self.cnt = {e: 0 for e in self.ENGS}
        self.dcnt = {}
        self.limit = int(os.environ.get('KLIMIT', '0')) or None
        self.force = False

    def _cls(self, op):
        return ("dma", op["dma"]) if op["dma"] is not None else ("eng", op["eng"])

    def op(self, eng, emit, reads=(), writes=(), dma=None):
        i = len(self.ops)
        if self.limit is not None and i >= self.limit and not self.force:
            return
        o = dict(eng=eng, emit=emit, dma=dma, tok=None, need=False)
        cls = self._cls(o)
        rb = [v.blocks for v in reads if isinstance(v, View)]
        wb = [v.blocks for v in writes if isinstance(v, View)]
        R = np.unique(np.concatenate(rb)) if rb else np.zeros(0, dtype=np.int64)
        Wb = np.unique(np.concatenate(wb)) if wb else np.zeros(0, dtype=np.int64)
        if R.size and (R >= PS_OFF).any():
            Wb = np.unique(np.concatenate([Wb, R[R >= PS_OFF]]))
        deps = {}

        def consider(arr, kind):
            for p in np.unique(arr):
                p = int(p)
                if p < 0 or p == i:
                    continue
                po = self.ops[p]
                pc = self._cls(po)
                if pc == cls and cls[0] == "eng":
                    if eng == "pe":
                        continue
                    if kind != "raw":
                        continue
                if pc not in deps or deps[pc] < p:
                    deps[pc] = p

        if R.size:
            consider(self.last_w[R], "raw")
        if Wb.size:
            consider(self.last_w[Wb], "waw")
            for c, arr in self.last_r.items():
                consider(arr[Wb], "war")
        if R.size:
            if cls not in self.last_r:
                self.last_r[cls] = np.full(self.nb, -1, dtype=np.int64)
            self.last_r[cls][R] = i
        if Wb.size:
            self.last_w[Wb] = i
            for c, arr in self.last_r.items():
                arr[Wb] = -1
        o["deps"] = sorted(deps.values())
        for p in o["deps"]:
            self.ops[p]["need"] = True
        if dma is not None:
            self.dcnt[dma] = self.dcnt.get(dma, 0) + 1
            o["dn"] = self.dcnt[dma]
        self.ops.append(o)

    def emit(self, final_wait_keys=()):
        nc = self.nc
        ops = self.ops
        cnt = {e: 0 for e in self.ENGS}
        for o in ops:
            if o["dma"] is not None:
                o["tok"] = (("dma", o["dma"]), 16 * o["dn"])
            elif o["need"]:
                cnt[o["eng"]] += 1
                o["tok"] = (("eng", o["eng"]), cnt[o["eng"]])
        dma_keys = sorted(self.dcnt)
        with contextlib.ExitStack() as st:
            sems = {}
            for e in self.ENGS:
                sems[("eng", e)] = st.enter_context(nc.semaphore("s_" + e))
            for k in dma_keys:
                sems[("dma", k)] = st.enter_context(nc.semaphore("d_" + k))
            block = st.enter_context(nc.Block())

            def run(engname, eng):
                waited = {}
                for o in ops:
                    if o["eng"] != engname:
                        continue
                    for p in o["deps"]:
                        skey, val = ops[p]["tok"]
                        if waited.get(skey, 0) >= val:
                            continue
                        eng.wait_ge(sems[skey], val)
                        waited[skey] = val
                    ins = o["emit"](eng)
                    if o["tok"] is not None:
                        skey, val = o["tok"]
                        ins.then_inc(sems[skey], 16 if skey[0] == "dma" else 1)
                if engname == "sp":
                    for k in final_wait_keys:
                        eng.wait_ge(sems[("dma", k)], 16 * self.dcnt[k])

            @block.tensor
            def _(e):
                run("pe", e)

            @block.scalar
            def _(e):
                run("act", e)

            @block.vector
            def _(e):
                run("dve", e)

            @block.gpsimd
            def _(e):
                run("pool", e)

            @block.sync
            def _(e):
                run("sp", e)

    def mm(self, out, lhsT, rhs, start=True, stop=True):
        self.op("pe", lambda e: e.matmul(out.ap, lhsT=lhsT.ap, rhs=rhs.ap, start=start, stop=stop),
                [lhsT, rhs], [out])

    def tr(self, out, in_, ident):
        self.op("pe", lambda e: e.transpose(out.ap, in_.ap, ident.ap), [in_, ident], [out])

    def act(self, out, in_, func, bias=None, scale=None, accum=None):
        kw = {}
        reads = [in_]
        if bias is not None:
            kw["bias"] = bias.ap if isinstance(bias, View) else bias
            reads.append(bias)
        if scale is not None:
            kw["scale"] = scale.ap if isinstance(scale, View) else scale
            reads.append(scale)
        writes = [out]
        if accum is not None:
            kw["accum_out"] = accum.ap
            writes.append(accum)
        self.op("act", lambda e: e.activation(out=out.ap, in_=in_.ap, func=func, **kw), reads, writes)

    def ts(self, eng, out, in0, s1, s2=None, op0=ALU.mult, op1=None):
        a1 = s1.ap if isinstance(s1, View) else s1
        a2 = s2.ap if isinstance(s2, View) else s2
        kw = {}
        if op1 is not None:
            kw["op1"] = op1
        self.op(eng, lambda e: e.tensor_scalar(out=out.ap, in0=in0.ap, scalar1=a1, scalar2=a2, op0=op0, **kw),
                [in0, s1, s2], [out])

    def stt(self, out, in0, scalar, in1, op0, op1):
        a = scalar.ap if isinstance(scalar, View) else scalar
        self.op("dve", lambda e: e.scalar_tensor_tensor(out=out.ap, in0=in0.ap, scalar=a, in1=in1.ap, op0=op0, op1=op1),
                [in0, scalar, in1], [out])

    def tt(self, eng, out, in0, in1, op, in1_ap=None, in0_ap=None):
        b = in1_ap if in1_ap is not None else in1.ap
        a = in0_ap if in0_ap is not None else in0.ap
        self.op(eng, lambda e: e.tensor_tensor(out=out.ap, in0=a, in1=b, op=op), [in0, in1], [out])

    def cp(self, eng, out, in_):
        if eng == "act":
            self.op("act", lambda e: e.copy(out=out.ap, in_=in_.ap), [in_], [out])
        else:
            self.op(eng, lambda e: e.tensor_copy(out=out.ap, in_=in_.ap), [in_], [out])

    def memset(self, eng, out, val):
        self.op(eng, lambda e: e.memset(out.ap, val), [], [out])

    def recip(self, out, in_):
        self.op("dve", lambda e: e.reciprocal(out=out.ap, in_=in_.ap), [in_], [out])

    def dma(self, q, out, in_, key):
        oa = out.ap if isinstance(out, View) else out
        ia = in_.ap if isinstance(in_, View) else in_
        self.op(q, lambda e: e.dma_start(out=oa, in_=ia), [in_], [out], dma=key)


COL = {}
_c = 0
for _n in ("a_norm1", "a_norm2", "kv_norm", "b_norm1", "b_norm2", "final_norm", "a_mem_norm", "b_mem_norm"):
    COL[_n] = _c
    _c += 8
for _l in ("a", "b"):
    for _n in ("cw0", "cw1", "cw2", "cb"):
        COL[_l + "_" + _n] = _c
        _c += 44
COL["a_b_s"] = _c
_c += 4
NCOL = _c

WEIGHTS = {
    "a_w_in": (D, 1792), "a_w_mem_kv": (D, 512), "a_w_out": (D, D), "a_ffn_in": (D, 2 * DFF), "a_ffn_out": (DFF, D),
    "w_kv": (D, 1548), "b_w_q": (D, D), "b_w_mem_kv": (D, 512), "b_w_out": (D, D), "b_ffn_in": (D, 2 * DFF),
    "b_ffn_out": (DFF, D),
}


def build_program(upto=6):
    nc = bass.Bass("TRN2", target_bir_lowering=False)

    def din(name, shape):
        return nc.dram_tensor(name, list(shape), F32, kind="ExternalInput").ap()

    x_d = din("x", (S, D))
    mem_d = din("mem", (NMEM, D))
    Wd = {n: din(n, s) for n, s in WEIGHTS.items()}
    ws_d = din("a_w_s", (4, 128, 128))
    cols_d = din("cols", (128, NCOL))
    consts_d = din("consts", (128, 6, 128))
    vnbc_d = din("vnbc", (128, TOK))
    bfbc_d = din("bfbc", (128, 16))
    out_d = nc.dram_tensor("out", [S, D], F32, kind="ExternalOutput").ap()

    st = contextlib.ExitStack()
    with st:
        ARENA = 212736
        arena = st.enter_context(nc.sbuf_tensor("arena", [128, ARENA // 2], BF16))
        pst = [st.enter_context(nc.psum_tensor("ps%d" % j, [128, 1024], F32)) for j in range(4)]

        def T(off, shape, dt):
            es = 4 if dt == F32 else 2
            n = int(np.prod(shape))
            assert off % 4 == 0 and off + n * es <= ARENA, (off, shape)
            ap = arena[:, off // 2:(off + n * es) // 2]
            if dt == F32:
                ap = ap.bitcast(F32)
            if len(shape) == 2:
                ap = ap.rearrange("p (a b) -> p a b", a=shape[0])
            elif len(shape) == 3:
                ap = ap.rearrange("p (a b c) -> p a b c", a=shape[0], b=shape[1])
            elif len(shape) == 4:
                ap = ap.rearrange("p (a b c d) -> p a b c d", a=shape[0], b=shape[1], c=shape[2])
            return Tile(ap, "sb", off, shape, es)

        def bank(b, dt=F32):
            ap = pst[b // 2][:, (b % 2) * 512:(b % 2 + 1) * 512]
            base = (b // 2) * 4096 + (b % 2) * 2048
            if dt == BF16:
                return Tile(ap.bitcast(BF16), "ps", base, (1024,), 2)
            return Tile(ap, "ps", base, (512,), 4)

        def bank2(j):
            return Tile(pst[j][:, :], "ps", j * 4096, (1024,), 4)

        o = 0
        xT = T(o, (KC, S), F32); o += 65536
        HT0 = o; hT = T(o, (2, KC, 512), BF16); o += 16384
        BIG = o; o += 49536
        WA = o; o += 28672
        WB = o; o += 16384
        CAT = o; cat = T(o, (KC, 512), BF16); o += 8192
        sm = [o]

        def small(shape, dt):
            es = 4 if dt == F32 else 2
            n = int(np.prod(shape)) * es
            n = (n + 63) // 64 * 64
            t = T(sm[0], shape, dt)
            sm[0] += n
            return t

        consts = small((6, 128), F32)
        cols = small((NCOL,), F32)
        identb = small((128,), BF16)
        maskb = small((128,), BF16)
        onesb = small((128,), BF16)
        halo = small((44, 2), F32)
        cfox = small((NT, 12), F32)
        rbc = small((NT, 12), F32)
        rstd = small((2, 512), F32)
        sq = small((2, 512), BF16)
        scr = small((64,), F32)
        bfbc = small((16,), F32)
        KmTs = {l: small((2, NMEM), BF16) for l in 'ab'}
        Vms = {l: small((2, 4, 65), BF16) for l in 'ab'}
        ptr_ = small((4, 512), BF16)
        tokblk = small((1024,), BF16)
        bias_i = small((2, NT, 12), F32)
        orec = small((16,), F32)
        vprime = small((16, 65), BF16)
        assert sm[0] <= ARENA, sm[0]

        stage = T(BIG, (2, 1024), F32)
        memst = T(BIG + 8192, (1024,), F32)
        memh = T(BIG + 12288, (1024,), BF16)
        hmT = T(BIG + 16384, (KC, NMEM), BF16)
        lf = T(CAT, (NT, 12), F32)
        lfx = T(CAT + 1024, (NT, 12), F32)
        KT = T(BIG, (6, S), BF16)
        Vaug = T(BIG + 24576, (NT, 12, 65), BF16)
        gT = T(BIG, (NCH, 1024), BF16)
        a_wmkv = T(BIG + 24576, (KC, 512), BF16)
        a_hmg = T(BIG + 32768, (KC, NMEM), BF16)
        vnbc = T(BIG + 12288, (TOK,), F32)
        u_t = T(BIG + 15360, (TOK,), F32)
        v_t = T(BIG + 18432, (TOK,), F32)
        vn_t = T(BIG + 21504, (TOK,), BF16)
        wsT = T(BIG + 23040, (4, 128), BF16)
        wstage = T(BIG + 24064, (4, 128), F32)
        wmask = T(BIG + 26112, (4, 128), BF16)
        a_qmT = T(BIG + 27136, (2, 512), BF16)
        a_qmT2 = [a_qmT, T(BIG + 29184, (2, 512), BF16)]
        u_t2 = [u_t, T(BIG + 36864, (TOK,), F32)]
        v_t2 = [v_t, T(BIG + 39936, (TOK,), F32)]
        vn_t2 = [vn_t, T(BIG + 43008, (TOK,), BF16)]
        tokb2 = [tokblk, T(BIG + 44544, (1024,), BF16)]
        a_win = T(WA, (KC, 1792), BF16)
        wkv = T(WA, (KC, 1548), BF16)
        b_wq = T(WA, (KC, D), BF16)
        b_wmkv = T(WA + 16384, (KC, 512), BF16)
        wout = T(WB, (KC, D), BF16)
        fwin = T(WA, (2, KC, 512), BF16)
        fwout = T(WA + 16384, (2, NCH, 128), BF16)
        hc = T(WB, (2, 2, 1024), F32)
        b_hmg = T(CAT, (KC, NMEM), BF16)
        QTzf = [T(WA + 16384 + z * 4096, (6, 2, 128), BF16) for z in range(2)]
        QTzm = [T(WA + 16384 + z * 4096 + 3072, (2, 2, 128), BF16) for z in range(2)]
        QTz_all = T(WA + 16384, (2 * 16 * 128,), BF16)
        vp = [vprime, T(WA + 24576, (16, 65), BF16)]
        eb_bf = T(WA + 24576 + 2112, (2, NT, 12), BF16)
        b_QT = T(HT0 + 8192, (6, 512), BF16)
        b_qmT = T(HT0 + 8192 + 6144, (2, 512), BF16)

        P = Prog(nc)
        A = slice(None)

        def col(name, j=0):
            c = COL[name] + j
            return cols[A, c:c + 1]

        P.dma("sp", consts.all(), consts_d, "c0")
        P.dma("sp", cols.all(), cols_d, "c1")
        P.dma("sp", bfbc.all(), bfbc_d, "c2")
        P.cp("dve", identb.all(), consts[A, 0, A])
        P.cp("dve", maskb.all(), consts[A, 5, A])
        P.cp("dve", onesb.all(), consts[A, 4, A])
        identf = consts[A, 0, A]

        evac_rr = [0]

        def evac_eng():
            evac_rr[0] ^= 1
            return "dve" if evac_rr[0] else "act"

        def load_block(b, banks=None):
            sl = b % 2
            P.dma("sp", stage[A, sl, A], x_d[b * 128:(b + 1) * 128, :], "xs%d" % sl)
            for half in range(2):
                pb = bank(half + 2 * (b % 2)) if banks is None else bank(banks[half])
                for cc in range(4):
                    c = half * 4 + cc
                    P.tr(pb[A, cc * 128:(cc + 1) * 128], stage[A, sl, c * 128:(c + 1) * 128], identf)
                P.cp(evac_eng(), xT[A, half * 4:half * 4 + 4, b * 128:(b + 1) * 128],
                     View(pb.ap.rearrange("p (a b) -> p a b", a=4), pb.all().blocks))

        DEFER = upto >= 2
        for b in range(8 if DEFER else NT):
            load_block(b)

        for blk in range(2 if upto >= 1 else 0):
            P.dma("sp", memst.all(), mem_d[blk * 128:(blk + 1) * 128, :], "ms")
            P.act(memh.all(), memst.all(), AF.Square, accum=scr[A, 0:1])
            P.act(scr[A, 1:2], scr[A, 0:1], AF.Ln, scale=1.0 / D, bias=EPS)
            P.act(scr[A, 2:3], scr[A, 1:2], AF.Exp, scale=-0.5)
            P.ts("dve", memh.all(), memst.all(), scr[A, 2:3], None, op0=ALU.mult)
            pb = bank(7, BF16)
            for c in range(KC):
                P.tr(pb[A, c * 128:(c + 1) * 128], memh[A, c * 128:(c + 1) * 128], identb.all())
            P.cp("dve", hmT[A, A, blk * 128:(blk + 1) * 128],
                 View(pb.ap.rearrange("p (a b) -> p a b", a=KC), pb.all().blocks))

        def wload(tile_view, dram_ap, key):
            P.dma("pool", tile_view, dram_ap, key)

        def wv(name):
            return Wd[name].rearrange("(k p) f -> p k f", p=128)

        def mem_kv(layer, wmkv, hmg):
            KmT, Vm = KmTs[layer], Vms[layer]
            wload(wmkv.all(), wv(layer + "_w_mem_kv"), "wmkv_" + layer)
            for c in range(KC):
                P.ts("dve", hmg[A, c, A], hmT[A, c, A], col(layer + "_mem_norm", c), None, op0=ALU.mult)
            for cc in range(2):
                pb = bank(6)
                for k in range(KC):
                    P.mm(pb[A, 0:NMEM], wmkv[A, k, cc * 128:(cc + 1) * 128], hmg[A, k, A], start=(k == 0), stop=(k == KC - 1))
                P.cp("dve", KmT[A, cc, A], pb[A, 0:NMEM])
            P.memset("pool", Vm[A, A, A, 64:65], 1.0)
            for blk in range(2):
                pb = bank(6)
                for k in range(KC):
                    P.mm(pb[A, 0:256], hmg[A, k, blk * 128:(blk + 1) * 128], wmkv[A, k, 256:512], start=(k == 0), stop=(k == KC - 1))
                P.cp("dve", Vm[A, blk, A, 0:64], View(pb.ap[:, 0:256].rearrange("p (a b) -> p a b", a=4), pb[A, 0:256].blocks))

        if upto >= 1:
            mem_kv("b", b_wmkv, b_hmg)
            mem_kv("a", a_wmkv, a_hmg)

        def norm_tile(name, t0, slot, rs):
            pb = bank(7)
            for c in range(KC):
                P.act(sq[A, c % 2, A], xT[A, c, t0:t0 + 512], AF.Square)
                P.mm(pb.all(), onesb.all(), sq[A, c % 2, A], start=(c == 0), stop=(c == KC - 1))
            P.act(rstd[A, rs, A], pb.all(), AF.Ln, scale=1.0 / D, bias=EPS)
            P.act(rstd[A, rs, A], rstd[A, rs, A], AF.Exp, scale=-0.5)
            for c in range(KC):
                P.stt(hT[A, slot, c, A], xT[A, c, t0:t0 + 512], col(name, c), rstd[A, rs, A], ALU.mult, ALU.mult)

        att_state = dict(n=0)
        vp_state = [0]
        NVP = 16

        def attention(groups, sring=(0, 1), L=1, after_qk=None):
            chunks = []
            for tiles in groups:
                chunks += [tiles[i:i + 4] for i in range(0, len(tiles), 4)]
            info = []

            def qk(ci):
                n = att_state["n"]
                att_state["n"] += 1
                sb_ = bank(sring[n % len(sring)])
                slot = n % 4
                info.append((sb_, slot))
                for jj, t in enumerate(chunks[ci]):
                    o_ = sb_[A, jj * 128:(jj + 1) * 128]
                    P.mm(o_, t["lhsT"], t["rhs"], start=True, stop=not t["mask"])
                    if t["mask"]:
                        P.mm(o_, identb.all(), maskb.all(), start=False, stop=True)

            def ex_pv(ci):
                sb_, slot = info[ci]
                n = len(chunks[ci])
                P.act(ptr_[A, slot, 0:n * 128], sb_[A, 0:n * 128], AF.Exp, scale=0.125)
                for jj, t in enumerate(chunks[ci]):
                    if t.get("pre") is not None:
                        t["pre"]()
                    rhs_ = t["pv_rhs"]
                    if t.get("vs") is not None:
                        vslot = vp_state[0] % NVP
                        vp_state[0] += 1
                        eng_ = "dve"
                        if eng_ == "pool":
                            P.ts("pool", vprime[A, vslot, A], rhs_, t["vs"], 1.0, op0=ALU.mult, op1=ALU.mult)
                        else:
                            P.ts("dve", vprime[A, vslot, A], rhs_, t["vs"], None, op0=ALU.mult)
                        rhs_ = vprime[A, vslot, A]
                    P.mm(t["pv_out"], ptr_[A, slot, jj * 128:(jj + 1) * 128], rhs_, start=t["start"], stop=t["stop"])

            for ci in range(min(L, len(chunks))):
                qk(ci)
            if after_qk is not None:
                after_qk()
            for ci in range(len(chunks)):
                if ci + L < len(chunks):
                    qk(ci + L)
                ex_pv(ci)

        def mem_tiles(layer, qmT, qi, ombank, qz=None):
            KmT, Vm = KmTs[layer], Vms[layer]
            tls = [[], []]
            for h in (0, 2, 1, 3):
                cc, po = h // 2, (h % 2) * 64
                tl = tls[h % 2] if qz is None else tls[0]
                for kb in range(2):
                    if qz is None:
                        l_, r_ = KmT[po:po + 64, cc, kb * 128:(kb + 1) * 128], qmT[po:po + 64, cc, qi * 128:(qi + 1) * 128]
                    else:
                        l_, r_ = KmT[A, cc, kb * 128:(kb + 1) * 128], qz[A, cc, h % 2, A]
                    tl.append(dict(lhsT=l_, rhs=r_,
                                   mask=False, vs=None, pv_rhs=Vm[A, kb, h, A], pv_out=ombank[A, h * 65:(h + 1) * 65],
                                   start=(kb == 0), stop=(kb == 1)))
            return tls

        def mem_finish(ombank, tokblk=tokblk):
            ov = View(ombank.ap[:, 0:260].rearrange("p (a b) -> p a b", a=4), ombank[A, 0:260].blocks)
            P.recip(orec[A, 12:16], View(ov.ap[:, :, 64], ov.blocks))
            P.tt("dve", View(tokblk.ap[:, 768:1024].rearrange("p (a b) -> p a b", a=4), tokblk[A, 768:1024].blocks),
                 View(ov.ap[:, :, 0:64], ov.blocks), orec[A, 12:16], ALU.mult,
                 in1_ap=orec.ap[:, 12:16].unsqueeze(2).broadcast_to([128, 4, 64]))

        def cat_block(qi, tokblk=tokblk):
            pb = bank(7, BF16)
            for c in range(KC):
                P.tr(pb[A, c * 128:(c + 1) * 128], tokblk[A, c * 128:(c + 1) * 128], identb.all())
            P.cp("dve", cat[A, A, qi * 128:(qi + 1) * 128], View(pb.ap.rearrange("p (a b) -> p a b", a=KC), pb.all().blocks))

        def out_proj(t0):
            for dc in range(KC):
                pb = bank(6 + (dc % 2))
                for k in range(KC):
                    P.mm(pb.all(), wout[A, k, dc * 128:(dc + 1) * 128], cat[A, k, A], start=(k == 0), stop=(k == KC - 1))
                P.tt("dve", xT[A, dc, t0:t0 + 512], pb.all(), xT[A, dc, t0:t0 + 512], ALU.add)

        def ffn(layer):
            wi = wv(layer + "_ffn_in")
            wo = Wd[layer + "_ffn_out"].rearrange("(c p) d -> p c d", p=128)
            nq = [0]
            ns = [0]

            def load_quad(q):
                sl = nq[0] % 2
                nq[0] += 1
                wload(fwin[A, sl, A, 0:256], wi[:, :, q * 256:(q + 1) * 256], "fwg%d" % sl)
                wload(fwin[A, sl, A, 256:512], wi[:, :, DFF + q * 256:DFF + (q + 1) * 256], "fwu%d" % sl)
                return sl

            def load_slab(dc):
                sl = ns[0] % 2
                ns[0] += 1
                wload(fwout[A, sl, A, A], wo[:, :, dc * 128:(dc + 1) * 128], "fwo%d" % sl)
                return sl

            nchunk = [0]
            for tt_ in range(2):
                norm_tile(layer + "_norm2", tt_ * 512, tt_, tt_)
            for half in range(2):
                h0 = half * 1024
                pend = load_quad(0)
                for q in range(11):
                    sl = pend
                    if q + 1 < 11:
                        pend = load_quad(q + 1)
                    for sub in range(2):
                        c = 2 * q + sub
                        buf = c % 2
                        for kind in range(2):
                            ch = c + kind * NCH
                            pj = nchunk[0] % 3
                            nchunk[0] += 1
                            pb2 = bank2(pj)
                            wc = kind * 256 + sub * 128
                            for k in range(KC):
                                for tb in range(2):
                                    P.mm(pb2[A, tb * 512:(tb + 1) * 512], fwin[A, sl, k, wc:wc + 128], hT[A, tb, k, A],
                                         start=(k == 0), stop=(k == KC - 1))
                            h_ = hc[A, buf, kind, A]
                            P.act(h_, pb2.all(), AF.Identity, bias=col(layer + "_cb", ch), scale=col(layer + "_cw2", ch))
                            P.stt(hc[A, buf, kind, 1:1024], pb2[A, 0:1023], col(layer + "_cw1", ch), hc[A, buf, kind, 1:1024], ALU.mult, ALU.add)
                            P.stt(hc[A, buf, kind, 2:1024], pb2[A, 0:1022], col(layer + "_cw0", ch), hc[A, buf, kind, 2:1024], ALU.mult, ALU.add)
                            if half == 1:
                                P.stt(hc[A, buf, kind, 0:1], halo[A, ch, 1:2], col(layer + "_cw1", ch), hc[A, buf, kind, 0:1], ALU.mult, ALU.add)
                                P.stt(hc[A, buf, kind, 0:2], halo[A, ch, 0:2], col(layer + "_cw0", ch), hc[A, buf, kind, 0:2], ALU.mult, ALU.add)
                            else:
                                P.cp("dve", halo[A, ch, A], pb2[A, 1022:1024])
                        P.act(hc[A, buf, 0, A], hc[A, buf, 0, A], AF.Silu)
                        P.tt("pool", gT[A, c, A], hc[A, buf, 0, A], hc[A, buf, 1, A], ALU.mult)
                if half == 0:
                    for tt_ in range(2):
                        norm_tile(layer + "_norm2", 1024 + tt_ * 512, tt_, tt_)
                pend = load_slab(0)
                for dc in range(KC):
                    sl = pend
                    if dc + 1 < KC:
                        pend = load_slab(dc + 1)
                    for tb in range(2):
                        pb = bank(6 + tb)
                        for c in range(NCH):
                            P.mm(pb.all(), fwout[A, sl, c, A], gT[A, c, tb * 512:(tb + 1) * 512], start=(c == 0), stop=(c == NCH - 1))
                        t0 = h0 + tb * 512
                        P.tt("dve", xT[A, dc, t0:t0 + 512], pb.all(), xT[A, dc, t0:t0 + 512], ALU.add)

        def layer_a():
            wload(a_win[A, A, 0:896], wv("a_w_in")[:, :, 0:896], "win0")
            wload(a_win[A, A, 896:1792], wv("a_w_in")[:, :, 896:1792], "win1")
            wload(wout.all(), wv("a_w_out"), "wout")
            P.dma("sp", vnbc.all(), vnbc_d, "c3")
            P.dma("sp", wstage.all(), ws_d.rearrange("g t s -> t g s"), "c4")
            P.tt("dve", wmask.all(), wstage.all(), consts[A, 1, A], ALU.mult,
                 in1_ap=consts.ap[:, 1:2, :].broadcast_to([128, 4, 128]))
            pb = bank(7, BF16)
            for g in range(4):
                P.tr(pb[A, g * 128:(g + 1) * 128], wmask[A, g, A], identb.all())
            P.cp("dve", wsT.all(), View(pb.ap[:, 0:512].rearrange("p (a b) -> p a b", a=4), pb[A, 0:512].blocks))

            def prep_tile(t):
                slot = t % 2
                norm_tile("a_norm1", t * 512, slot, slot)
                for cc in range(2):
                    pb = bank(6)
                    for k in range(KC):
                        P.mm(pb.all(), a_win[A, k, 1536 + cc * 128:1536 + (cc + 1) * 128], hT[A, slot, k, A], start=(k == 0), stop=(k == KC - 1))
                    P.cp("act", a_qmT2[slot][A, cc, A], pb.all())

            def zmm(b, ks):
                t, qi = b // 4, b % 4
                slot = t % 2
                zb = [bank(2), bank(3), bank(4)]
                for k in ks:
                    for n in range(3):
                        P.mm(zb[n].all(), hT[A, slot, k, qi * 128:(qi + 1) * 128], a_win[A, k, n * 512:(n + 1) * 512],
                             start=(k == 0), stop=(k == KC - 1))

            def zphase(b):
                zmm(b, range(KC))
                zact(b)

            def zact(b):
                u_, v_, vn_ = u_t2[b % 2], v_t2[b % 2], vn_t2[b % 2]
                zb = [bank(2), bank(3), bank(4)]
                P.act(u_[A, 0:512], zb[0].all(), AF.Gelu_apprx_tanh)
                P.act(u_[A, 512:768], zb[1][A, 0:256], AF.Gelu_apprx_tanh)
                P.act(v_[A, 0:256], zb[1][A, 256:512], AF.Gelu_apprx_tanh)
                P.act(v_[A, 256:768], zb[2].all(), AF.Gelu_apprx_tanh)
                P.act(vn_.all(), v_.all(), AF.Square, accum=scr[A, 4 + 4 * (b % 2):5 + 4 * (b % 2)])
                P.act(scr[A, 5 + 4 * (b % 2):6 + 4 * (b % 2)], scr[A, 4 + 4 * (b % 2):5 + 4 * (b % 2)], AF.Ln, scale=1.0 / TOK, bias=EPS)
                P.act(scr[A, 6 + 4 * (b % 2):7 + 4 * (b % 2)], scr[A, 5 + 4 * (b % 2):6 + 4 * (b % 2)], AF.Exp, scale=-0.5)
                P.stt(vn_.all(), v_.all(), scr[A, 6 + 4 * (b % 2):7 + 4 * (b % 2)], vnbc.all(), ALU.mult, ALU.mult)

            def m1phase(b):
                u_, vn_ = u_t2[b % 2], vn_t2[b % 2]
                tk = tokb2[b % 2]
                mb = bank(5)
                for gp in range(2):
                    for g2 in range(2):
                        g = gp * 2 + g2
                        P.mm(mb[A, g2 * 192:(g2 + 1) * 192], wsT[A, g, A], vn_[A, g * 192:(g + 1) * 192])
                    for g2 in range(2):
                        g = gp * 2 + g2
                        P.stt(tk[A, g * 192:(g + 1) * 192], mb[A, g2 * 192:(g2 + 1) * 192], col("a_b_s", g),
                              u_[A, g * 192:(g + 1) * 192], ALU.add, ALU.mult)

            def m2phase(b, zb_=None):
                t, qi = b // 4, b % 4
                tk = tokb2[b % 2]
                omb = bank(6)
                hook = (lambda: zmm(zb_, range(0, 4))) if zb_ is not None else None
                attention(mem_tiles('a', a_qmT2[t % 2], qi, omb), L=2, after_qk=hook)
                if zb_ is not None:
                    zmm(zb_, range(4, KC))
                mem_finish(omb, tk)
                cat_block(qi, tk)

            prep_tile(0)
            zphase(0)
            zphase(1)
            m1phase(0)
            for b in range(NT):
                if b < 8:
                    load_block(8 + b, banks=(7, 5))
                if b % 4 == 1 and b // 4 + 1 < 4:
                    prep_tile(b // 4 + 1)
                m2phase(b, b + 2 if b + 2 < NT else None)
                if b + 1 < NT:
                    m1phase(b + 1)
                if b + 2 < NT:
                    zact(b + 2)
                if b % 4 == 3:
                    out_proj((b // 4) * 512)

        def kv_phase():
            wload(wkv[A, A, 0:768], wv("w_kv")[:, :, 0:768], "win0")
            wload(wkv[A, A, 768:1548], wv("w_kv")[:, :, 768:1548], "win1")
            P.memset("pool", Vaug[A, A, A, 64:65], 1.0)
            norm_tile("kv_norm", 0, 0, 0)
            for t in range(4):
                t0 = t * 512
                slot = t % 2
                for hp in range(6):
                    pb = bank(hp % 2)
                    for k in range(KC):
                        P.mm(pb.all(), wkv[A, k, hp * 128:(hp + 1) * 128], hT[A, slot, k, A], start=(k == 0), stop=(k == KC - 1))
                    P.cp(evac_eng(), KT[A, hp, t0:t0 + 512], pb.all())
                if t + 1 < 4:
                    norm_tile("kv_norm", t0 + 512, (t + 1) % 2, (t + 1) % 2)
                for qi in range(4):
                    blk = t * 4 + qi
                    pa, pbk = bank(2 + 2 * (qi % 2)), bank(3 + 2 * (qi % 2))
                    for k in range(KC):
                        P.mm(pa[A, 0:384], hT[A, slot, k, qi * 128:(qi + 1) * 128], wkv[A, k, 768:1152], start=(k == 0), stop=(k == KC - 1))
                        P.mm(pbk[A, 0:396], hT[A, slot, k, qi * 128:(qi + 1) * 128], wkv[A, k, 1152:1548], start=(k == 0), stop=(k == KC - 1))
                    P.cp("act", Vaug[A, blk, 0:6, 0:64], View(pa.ap[:, 0:384].rearrange("p (a b) -> p a b", a=6), pa[A, 0:384].blocks))
                    P.cp("dve", Vaug[A, blk, 6:12, 0:64], View(pbk.ap[:, 0:384].rearrange("p (a b) -> p a b", a=6), pbk[A, 0:384].blocks))
                    P.tt("dve", scr[A, 16:28], pbk[A, 384:396], bfbc[A, 0:12], ALU.add)
                    P.act(scr[A, 32:44], scr[A, 16:28], AF.Exp, scale=-1.0)
                    P.act(scr[A, 48:60], scr[A, 32:44], AF.Ln, bias=1.0)
                    P.ts("dve", lf[A, blk, A], scr[A, 48:60], -1.0, None, op0=ALU.mult)
            P.memset("dve", lfx[A, 0, A], 0.0)
            for b in range(1, NT):
                P.tt("dve", lfx[A, b, A], lfx[A, b - 1, A], lf[A, b - 1, A], ALU.add)
            lf2 = View(lf.ap.rearrange("p a b -> p (a b)"), lf.all().blocks)
            lfx2 = View(lfx.ap.rearrange("p a b -> p (a b)"), lfx.all().blocks)
            pb = bank(6)
            P.mm(pb[A, 0:192], consts[A, 2, A], lf2, start=True, stop=False)
            P.mm(pb[A, 0:192], consts[A, 4, A], lfx2, start=False, stop=True)
            P.cp("dve", View(cfox.ap.rearrange("p a b -> p (a b)"), cfox.all().blocks), pb[A, 0:192])
            pb = bank(7)
            P.mm(pb[A, 0:192], consts[A, 3, A], lf2, start=True, stop=False)
            P.mm(pb[A, 0:192], consts[A, 4, A], lfx2, start=False, stop=True)
            P.cp("dve", View(rbc.ap.rearrange("p a b -> p (a b)"), rbc.all().blocks), pb[A, 0:192])

        def layer_b():
            wload(b_wq.all(), wv("b_w_q"), "win0")
            wload(wout.all(), wv("b_w_out"), "wout")
            P.memset("pool", QTz_all.all(), 0.0)
            def qproj():
                for hp in range(8):
                    pb = bank(6 + hp % 2)
                    for k in range(KC):
                        P.mm(pb.all(), b_wq[A, k, hp * 128:(hp + 1) * 128], hT[A, 0, k, A], start=(k == 0), stop=(k == KC - 1))
                    dst = b_QT[A, hp, A] if hp < 6 else b_qmT[A, hp - 6, A]
                    P.cp(evac_eng(), dst, pb.all())

            norm_tile("b_norm1", 0, 0, 0)
            qproj()

            def prologue(i):
                t, qi = i // 4, i % 4
                bi = i % 2
                P.tt("dve", bias_i[A, bi, 0:i + 1, A], rbc[A, i:i + 1, A], cfox[A, 0:i + 1, A], ALU.subtract,
                     in0_ap=rbc.ap[:, i:i + 1, :].broadcast_to([128, i + 1, 12]))
                P.act(eb_bf[A, bi, 0:i + 1, A], bias_i[A, bi, 0:i + 1, A], AF.Exp)
                for par in range(2):
                    rows = slice(par * 64, par * 64 + 64)
                    P.cp("dve", QTzf[bi][rows, A, par, A], b_QT[rows, A, qi * 128:(qi + 1) * 128])
                    P.cp("dve", QTzm[bi][rows, A, par, A], b_qmT[rows, A, qi * 128:(qi + 1) * 128])
                tl = []
                vscales = []
                obanks = [bank(2), bank(3)]
                for h in (0, 2, 4, 6, 8, 10, 1, 3, 5, 7, 9, 11):
                    hp = h // 2
                    ob = obanks[h % 2]
                    hc_ = h // 2
                    vb = vp[vp_state[0] % 2]
                    vp_state[0] += 1

                    def vscale(vb=vb, h=h):
                        P.tt("dve", vb[A, 0:i + 1, A], Vaug[A, 0:i + 1, h, A], eb_bf[A, bi, 0:i + 1, h], ALU.mult,
                             in1_ap=eb_bf.ap[:, bi, 0:i + 1, h].unsqueeze(2).broadcast_to([128, i + 1, 65]))
                    vscales.append(vscale)
                    for j in range(i + 1):
                        tl.append(dict(lhsT=KT[A, hp, j * 128:(j + 1) * 128], rhs=QTzf[bi][A, hp, h % 2, A],
                                       mask=(j == i), vs=None, pv_rhs=vb[A, j, A],
                                       pv_out=ob[A, hc_ * 65:(hc_ + 1) * 65], start=(j == 0), stop=(j == i)))
                vscales[0]()
                for n in range(12):
                    if n + 1 < 12:
                        tl[n * (i + 1)]["pre"] = vscales[n + 1]
                omb = bank(6)
                tl = tl + mem_tiles('b', b_qmT, qi, omb, qz=QTzm[bi])[0]
                return tl, obanks, omb

            def epilogue(i, obanks, omb):
                qi = i % 4
                for par in range(2):
                    ob = obanks[par]
                    ov = View(ob.ap[:, 0:390].rearrange("p (a b) -> p a b", a=6), ob[A, 0:390].blocks)
                    P.recip(orec[A, par * 6:par * 6 + 6], View(ov.ap[:, :, 64], ov.blocks))
                    tv = tokblk.ap[:, 0:768].rearrange("p (a b c) -> p a b c", a=6, b=2)[:, :, par, :]
                    P.tt("dve", View(tv, tokblk[A, 0:768].blocks), View(ov.ap[:, :, 0:64], ov.blocks), orec[A, par * 6:par * 6 + 6], ALU.mult,
                         in1_ap=orec.ap[:, par * 6:par * 6 + 6].unsqueeze(2).broadcast_to([128, 6, 64]))
                mem_finish(omb)
                cat_block(qi)

            nxt = prologue(0)
            for i in range(NT):
                t, qi = i // 4, i % 4
                tl, obanks, omb = nxt
                attention([tl], sring=(0, 1, 4, 5), L=2)
                if qi == 0 and t + 1 < 4:
                    norm_tile("b_norm1", (t + 1) * 512, 0, (t + 1) % 2)
                if i + 1 < NT:
                    nxt = prologue(i + 1)
                epilogue(i, obanks, omb)
                if qi == 2 and t + 1 < 4:
                    qproj()
                if qi == 3:
                    out_proj(t * 512)

        if upto >= 2:
            layer_a()
        if upto >= 3:
            ffn("a")
        if upto >= 4:
            kv_phase()
        if upto >= 5:
            layer_b()
        if upto >= 6:
            ffn("b")

        P.force = True
        for t in range(4):
            t0 = t * 512
            pb = bank(7)
            for c in range(KC):
                P.act(sq[A, c % 2, A], xT[A, c, t0:t0 + 512], AF.Square)
                P.mm(pb.all(), onesb.all(), sq[A, c % 2, A], start=(c == 0), stop=(c == KC - 1))
            P.act(rstd[A, 0, A], pb.all(), AF.Ln, scale=1.0 / D, bias=EPS)
            P.act(rstd[A, 0, A], rstd[A, 0, A], AF.Exp, scale=-0.5)
            for c in range(KC):
                P.stt(xT[A, c, t0:t0 + 512], xT[A, c, t0:t0 + 512], col("final_norm", c), rstd[A, 0, A], ALU.mult, ALU.mult)
            for qi in range(4):
                b = t * 4 + qi
                sl = b % 2
                for half in range(2):
                    pbk = bank(half + 2 * sl)
                    for cc in range(4):
                        c = half * 4 + cc
                        P.tr(pbk[A, cc * 128:(cc + 1) * 128], xT[A, c, b * 128:(b + 1) * 128], identf)
                    P.cp(evac_eng(), stage[A, sl, half * 512:(half + 1) * 512], pbk.all())
                P.dma("sp", out_d[b * 128:(b + 1) * 128, :], stage[A, sl, A], "os%d" % sl)

        P.emit(final_wait_keys=["os0", "os1"])
    return nc


def _host_consts():
    c = np.zeros((128, 6, 128), np.float32)
    i = np.arange(128)
    c[:, 0, :] = np.eye(128, dtype=np.float32)
    c[:, 1, :] = (i[:, None] >= i[None, :])
    c[:, 2, :] = (i[:, None] <= i[None, :])
    c[:, 3, :] = (i[:, None] <= 64)
    c[:, 4, :] = 1.0
    c[:, 5, :] = np.where(i[:, None] > i[None, :], -30000.0, 0.0)
    return c


_NC_CACHE = {}


def kernel(**inp):
    f32 = lambda a: np.ascontiguousarray(np.asarray(a, dtype=np.float32))
    cols = np.zeros((128, NCOL), np.float32)

    def put(name, vec):
        v = f32(vec).reshape(-1, 128).T
        cols[:, COL[name]:COL[name] + v.shape[1]] = v

    put("a_norm1", inp["a_norm1"][0]); put("a_norm2", inp["a_norm2"][0]); put("kv_norm", inp["kv_norm"])
    put("b_norm1", inp["b_norm1"][0]); put("b_norm2", inp["b_norm2"][0]); put("final_norm", inp["final_norm"])
    put("a_mem_norm", inp["a_mem_norm"][0]); put("b_mem_norm", inp["b_mem_norm"][0])
    for l in ("a", "b"):
        cw = f32(inp[l + "_ffn_conv"])[0]
        for j in range(3):
            put("%s_cw%d" % (l, j), cw[j])
        put(l + "_cb", inp[l + "_ffn_conv_b"][0])
    put("a_b_s", f32(inp["a_b_s"])[0].reshape(-1))
    shared = {
        "cols": cols, "consts": _host_consts(),
        "vnbc": np.ascontiguousarray(np.broadcast_to(f32(inp["a_v_norm"])[0][None, :], (128, TOK))),
        "bfbc": np.ascontiguousarray(np.broadcast_to(np.pad(f32(inp["b_f"]), (0, 4))[None, :], (128, 16))),
        "a_w_s": f32(inp["a_w_s"])[0],
    }
    for n in WEIGHTS:
        a = f32(inp[n])
        shared[n] = a[0] if a.ndim == 3 else a
    x = f32(inp["x"])
    mem = f32(inp["mem"])
    if "nc" not in _NC_CACHE:
        _NC_CACHE["nc"] = build_program()
    nc = _NC_CACHE["nc"]
    in_maps = []
    for c in range(NCORES):
        m = dict(shared)
        m["x"] = x[c]
        m["mem"] = mem[c]
        in_maps.append(m)
    res = run_bass_kernel_spmd(nc, in_maps, core_ids=list(range(NCORES)))
    return np.stack([np.asarray(r["out"], dtype=np.float32) for r in res.results], axis=0)
```

```python
import contextlib
import os
import numpy as np
import concourse.bass as bass
import concourse.mybir as mybir
from concourse.bass_utils import run_bass_kernel_spmd

F32 = mybir.dt.float32
BF16 = mybir.dt.bfloat16
AF = mybir.ActivationFunctionType
ALU = mybir.AluOpType

S = 2048
D = 1024
NT = 16
KC = 8
NMEM = 256
TOK = 768
DFF = 2816
NCH = 22
EPS = 1e-6
NCORES = 8

SB_BLK = 64
PS_BLK = 2048
PS_OFF = 1 << 14


class View:
    __slots__ = ("ap", "blocks")

    def __init__(self, ap, blocks):
        self.ap = ap
        self.blocks = blocks


class Tile:
    def __init__(self, ap, space, base, shape, esize):
        self.ap = ap
        self.space = space
        self.shape = tuple(shape)
        self.esize = esize
        n = int(np.prod(shape))
        self.offs = (base + np.arange(n, dtype=np.int64) * esize).reshape(shape)

    def __getitem__(self, idx):
        if not isinstance(idx, tuple):
            idx = (idx,)
        f = idx[1:]
        sub = self.offs[f] if f else self.offs
        sub = np.asarray(sub).ravel()
        blk = SB_BLK if self.space == "sb" else PS_BLK
        lo = sub // blk
        hi = (sub + self.esize - 1) // blk
        b = np.unique(np.concatenate([lo, hi]))
        if self.space == "ps":
            b = b + PS_OFF
        return View(self.ap[idx], b)

    def all(self):
        return self[(slice(None),)]


class Prog:
    ENGS = ("pe", "act", "dve", "pool", "sp")

    def __init__(self, nc):
        self.nc = nc
        self.ops = []
        nb = PS_OFF + 512
        self.last_w = np.full(nb, -1, dtype=np.int64)
        self.last_r = {}
        self.nb = nb
        self.cnt = {e: 0 for e in self.ENGS}
        self.dcnt = {}
        self.limit = int(os.environ.get('KLIMIT', '0')) or None
        self.force = False

    def _cls(self, op):
        return ("dma", op["dma"]) if op["dma"] is not None else ("eng", op["eng"])

    def op(self, eng, emit, reads=(), writes=(), dma=None):
        i = len(self.ops)
        if self.limit is not None and i >= self.limit and not self.force:
            return
        o = dict(eng=eng, emit=emit, dma=dma, tok=None, need=False)
        cls = self._cls(o)
        rb = [v.blocks for v in reads if isinstance(v, View)]
        wb = [v.blocks for v in writes if isinstance(v, View)]
        R = np.unique(np.concatenate(rb)) if rb else np.zeros(0, dtype=np.int64)
        Wb = np.unique(np.concatenate(wb)) if wb else np.zeros(0, dtype=np.int64)
        if R.size and (R >= PS_OFF).any():
            Wb = np.unique(np.concatenate([Wb, R[R >= PS_OFF]]))
        deps = {}

        def consider(arr, kind):
            for p in np.unique(arr):
                p = int(p)
                if p < 0 or p == i:
                    continue
                po = self.ops[p]
                pc = self._cls(po)
                if pc == cls and cls[0] == "eng":
                    if eng == "pe":
                        continue
                    if kind != "raw":
                        continue
                if pc not in deps or deps[pc] < p:
                    deps[pc] = p

        if R.size:
            consider(self.last_w[R], "raw")
        if Wb.size:
            consider(self.last_w[Wb], "waw")
            for c, arr in self.last_r.items():
                consider(arr[Wb], "war")
        if R.size:
            if cls not in self.last_r:
                self.last_r[cls] = np.full(self.nb, -1, dtype=np.int64)
            self.last_r[cls][R] = i
        if Wb.size:
            self.last_w[Wb] = i
            for c, arr in self.last_r.items():
                arr[Wb] = -1
        o["deps"] = sorted(deps.values())
        for p in o["deps"]:
            self.ops[p]["need"] = True
        if dma is not None:
            self.dcnt[dma] = self.dcnt.get(dma, 0) + 1
            o["dn"] = self.dcnt[dma]
        self.ops.append(o)

    def emit(self, final_wait_keys=()):
        nc = self.nc
        ops = self.ops
        cnt = {e: 0 for e in self.ENGS}
        for o in ops:
            if o["dma"] is not None:
                o["tok"] = (("dma", o["dma"]), 16 * o["dn"])
            elif o["need"]:
                cnt[o["eng"]] += 1
                o["tok"] = (("eng", o["eng"]), cnt[o["eng"]])
        dma_keys = sorted(self.dcnt)
        with contextlib.ExitStack() as st:
            sems = {}
            for e in self.ENGS:
                sems[("eng", e)] = st.enter_context(nc.semaphore("s_" + e))
            for k in dma_keys:
                sems[("dma", k)] = st.enter_context(nc.semaphore("d_" + k))
            block = st.enter_context(nc.Block())

            def run(engname, eng):
                waited = {}
                for o in ops:
                    if o["eng"] != engname:
                        continue
                    for p in o["deps"]:
                        skey, val = ops[p]["tok"]
                        if waited.get(skey, 0) >= val:
                            continue
                        eng.wait_ge(sems[skey], val)
                        waited[skey] = val
                    ins = o["emit"](eng)
                    if o["tok"] is not None:
                        skey, val = o["tok"]
                        ins.then_inc(sems[skey], 16 if skey[0] == "dma" else 1)
                if engname == "sp":
                    for k in final_wait_keys:
                        eng.wait_ge(sems[("dma", k)], 16 * self.dcnt[k])

            @block.tensor
            def _(e):
                run("pe", e)

            @block.scalar
            def _(e):
                run("act", e)

            @block.vector
            def _(e):
                run("dve", e)

            @block.gpsimd
            def _(e):
                run("pool", e)

            @block.sync
            def _(e):
                run("sp", e)

    def mm(self, out, lhsT, rhs, start=True, stop=True):
        self.op("pe", lambda e: e.matmul(out.ap, lhsT=lhsT.ap, rhs=rhs.ap, start=start, stop=stop),
                [lhsT, rhs], [out])

    def tr(self, out, in_, ident):
        self.op("pe", lambda e: e.transpose(out.ap, in_.ap, ident.ap), [in_, ident], [out])

    def act(self, out, in_, func, bias=None, scale=None, accum=None):
        kw = {}
        reads = [in_]
        if bias is not None:
            kw["bias"] = bias.ap if isinstance(bias, View) else bias
            reads.append(bias)
        if scale is not None:
            kw["scale"] = scale.ap if isinstance(scale, View) else scale
            reads.append(scale)
        writes = [out]
        if accum is not None:
            kw["accum_out"] = accum.ap
            writes.append(accum)
        self.op("act", lambda e: e.activation(out=out.ap, in_=in_.ap, func=func, **kw), reads, writes)

    def ts(self, eng, out, in0, s1, s2=None, op0=ALU.mult, op1=None):
        a1 = s1.ap if isinstance(s1, View) else s1
        a2 = s2.ap if isinstance(s2, View) else s2
        kw = {}
        if op1 is not None:
            kw["op1"] = op1
        self.op(eng, lambda e: e.tensor_scalar(out=out.ap, in0=in0.ap, scalar1=a1, scalar2=a2, op0=op0, **kw),
                [in0, s1, s2], [out])

    def stt(self, out, in0, scalar, in1, op0, op1):
        a = scalar.ap if isinstance(scalar, View) else scalar
        self.op("dve", lambda e: e.scalar_tensor_tensor(out=out.ap, in0=in0.ap, scalar=a, in1=in1.ap, op0=op0, op1=op1),
                [in0, scalar, in1], [out])

    def tt(self, eng, out, in0, in1, op, in1_ap=None, in0_ap=None):
        b = in1_ap if in1_ap is not None else in1.ap
        a = in0_ap if in0_ap is not None else in0.ap
        self.op(eng, lambda e: e.tensor_tensor(out=out.ap, in0=a, in1=b, op=op), [in0, in1], [out])

    def cp(self, eng, out, in_):
        if eng == "act":
            self.op("act", lambda e: e.copy(out=out.ap, in_=in_.ap), [in_], [out])
        else:
            self.op(eng, lambda e: e.tensor_copy(out=out.ap, in_=in_.ap), [in_], [out])

    def memset(self, eng, out, val):
        self.op(eng, lambda e: e.memset(out.ap, val), [], [out])

    def recip(self, out, in_):
        self.op("dve", lambda e: e.reciprocal(out=out.ap, in_=in_.ap), [in_], [out])

    def dma(self, q, out, in_, key):
        oa = out.ap if isinstance(out, View) else out
        ia = in_.ap if isinstance(in_, View) else in_
        self.op(q, lambda e: e.dma_start(out=oa, in_=ia), [in_], [out], dma=key)


COL = {}
_c = 0
for _n in ("a_norm1", "a_norm2", "kv_norm", "b_norm1", "b_norm2", "final_norm", "a_mem_norm", "b_mem_norm"):
    COL[_n] = _c
    _c += 8
for _l in ("a", "b"):
    for _n in ("cw0", "cw1", "cw2", "cb"):
        COL[_l + "_" + _n] = _c
        _c += 44
COL["a_b_s"] = _c
_c += 4
NCOL = _c

WEIGHTS = {
    "a_w_in": (D, 1792), "a_w_mem_kv": (D, 512), "a_w_out": (D, D), "a_ffn_in": (D, 2 * DFF), "a_ffn_out": (DFF, D),
    "w_kv": (D, 1548), "b_w_q": (D, D), "b_w_mem_kv": (D, 512), "b_w_out": (D, D), "b_ffn_in": (D, 2 * DFF),
    "b_ffn_out": (DFF, D),
}


def build_program(upto=6):
    nc = bass.Bass("TRN2", target_bir_lowering=False)

    def din(name, shape):
        return nc.dram_tensor(name, list(shape), F32, kind="ExternalInput").ap()

    x_d = din("x", (S, D))
    mem_d = din("mem", (NMEM, D))
    Wd = {n: din(n, s) for n, s in WEIGHTS.items()}
    ws_d = din("a_w_s", (4, 128, 128))
    cols_d = din("cols", (128, NCOL))
    consts_d = din("consts", (128, 6, 128))
    vnbc_d = din("vnbc", (128, TOK))
    bfbc_d = din("bfbc", (128, 16))
    out_d = nc.dram_tensor("out", [S, D], F32, kind="ExternalOutput").ap()

    st = contextlib.ExitStack()
    with st:
        ARENA = 212736
        arena = st.enter_context(nc.sbuf_tensor("arena", [128, ARENA // 2], BF16))
        pst = [st.enter_context(nc.psum_tensor("ps%d" % j, [128, 1024], F32)) for j in range(4)]

        def T(off, shape, dt):
            es = 4 if dt == F32 else 2
            n = int(np.prod(shape))
            assert off % 4 == 0 and off + n * es <= ARENA, (off, shape)
            ap = arena[:, off // 2:(off + n * es) // 2]
            if dt == F32:
                ap = ap.bitcast(F32)
            if len(shape) == 2:
                ap = ap.rearrange("p (a b) -> p a b", a=shape[0])
            elif len(shape) == 3:
                ap = ap.rearrange("p (a b c) -> p a b c", a=shape[0], b=shape[1])
            elif len(shape) == 4:
                ap = ap.rearrange("p (a b c d) -> p a b c d", a=shape[0], b=shape[1], c=shape[2])
            return Tile(ap, "sb", off, shape, es)

        def bank(b, dt=F32):
            ap = pst[b // 2][:, (b % 2) * 512:(b % 2 + 1) * 512]
            base = (b // 2) * 4096 + (b % 2) * 2048
            if dt == BF16:
                return Tile(ap.bitcast(BF16), "ps", base, (1024,), 2)
            return Tile(ap, "ps", base, (512,), 4)

        def bank2(j):
            return Tile(pst[j][:, :], "ps", j * 4096, (1024,), 4)

        o = 0
        xT = T(o, (KC, S), F32); o += 65536
        HT0 = o; hT = T(o, (2, KC, 512), BF16); o += 16384
        BIG = o; o += 49536
        WA = o; o += 28672
        WB = o; o += 16384
        CAT = o; cat = T(o, (KC, 512), BF16); o += 8192
        sm = [o]

        def small(shape, dt):
            es = 4 if dt == F32 else 2
            n = int(np.prod(shape)) * es
            n = (n + 63) // 64 * 64
            t = T(sm[0], shape, dt)
            sm[0] += n
            return t

        consts = small((6, 128), F32)
        cols = small((NCOL,), F32)
        identb = small((128,), BF16)
        maskb = small((128,), BF16)
        onesb = small((128,), BF16)
        halo = small((44, 2), F32)
        cfox = small((NT, 12), F32)
        rbc = small((NT, 12), F32)
        rstd = small((2, 512), F32)
        sq = small((2, 512), BF16)
        scr = small((64,), F32)
        bfbc = small((16,), F32)
        KmTs = {l: small((2, NMEM), BF16) for l in 'ab'}
        Vms = {l: small((2, 4, 65), BF16) for l in 'ab'}
        ptr_ = small((4, 512), BF16)
        tokblk = small((1024,), BF16)
        bias_i = small((2, NT, 12), F32)
        orec = small((16,), F32)
        vprime = small((16, 65), BF16)
        assert sm[0] <= ARENA, sm[0]

        stage = T(BIG, (2, 1024), F32)
        memst = T(BIG + 8192, (1024,), F32)
        memh = T(BIG + 12288, (1024,), BF16)
        hmT = T(BIG + 16384, (KC, NMEM), BF16)
        lf = T(CAT, (NT, 12), F32)
        lfx = T(CAT + 1024, (NT, 12), F32)
        KT = T(BIG, (6, S), BF16)
        Vaug = T(BIG + 24576, (NT, 12, 65), BF16)
        gT = T(BIG, (NCH, 1024), BF16)
        a_wmkv = T(BIG + 24576, (KC, 512), BF16)
        a_hmg = T(BIG + 32768, (KC, NMEM), BF16)
        vnbc = T(BIG + 12288, (TOK,), F32)
        u_t = T(BIG + 15360, (TOK,), F32)
        v_t = T(BIG + 18432, (TOK,), F32)
        vn_t = T(BIG + 21504, (TOK,), BF16)
        wsT = T(BIG + 23040, (4, 128), BF16)
        wstage = T(BIG + 24064, (4, 128), F32)
        wmask = T(BIG + 26112, (4, 128), BF16)
        a_qmT = T(BIG + 27136, (2, 512), BF16)
        a_qmT2 = [a_qmT, T(BIG + 29184, (2, 512), BF16)]
        u_t2 = [u_t, T(BIG + 36864, (TOK,), F32)]
        v_t2 = [v_t, T(BIG + 39936, (TOK,), F32)]
        vn_t2 = [vn_t, T(BIG + 43008, (TOK,), BF16)]
        tokb2 = [tokblk, T(BIG + 44544, (1024,), BF16)]
        a_win = T(WA, (KC, 1792), BF16)
        wkv = T(WA, (KC, 1548), BF16)
        b_wq = T(WA, (KC, D), BF16)
        b_wmkv = T(WA + 16384, (KC, 512), BF16)
        wout = T(WB, (KC, D), BF16)
        fwin = T(WA, (2, KC, 512), BF16)
        fwout = T(WA + 16384, (2, NCH, 128), BF16)
        hc = T(WB, (2, 2, 1024), F32)
        b_hmg = T(CAT, (KC, NMEM), BF16)
        QTzf = [T(WA + 16384 + z * 4096, (6, 2, 128), BF16) for z in range(2)]
        QTzm = [T(WA + 16384 + z * 4096 + 3072, (2, 2, 128), BF16) for z in range(2)]
        QTz_all = T(WA + 16384, (2 * 16 * 128,), BF16)
        vp = [vprime, T(WA + 24576, (16, 65), BF16)]
        eb_bf = T(WA + 24576 + 2112, (2, NT, 12), BF16)
        b_QT = T(HT0 + 8192, (6, 512), BF16)
        b_qmT = T(HT0 + 8192 + 6144, (2, 512), BF16)

        P = Prog(nc)
        A = slice(None)

        def col(name, j=0):
            c = COL[name] + j
            return cols[A, c:c + 1]

        P.dma("sp", consts.all(), consts_d, "c0")
        P.dma("sp", cols.all(), cols_d, "c1")
        P.dma("sp", bfbc.all(), bfbc_d, "c2")
        P.cp("dve", identb.all(), consts[A, 0, A])
        P.cp("dve", maskb.all(), consts[A, 5, A])
        P.cp("dve", onesb.all(), consts[A, 4, A])
        identf = consts[A, 0, A]

        evac_rr = [0]

        def evac_eng():
            evac_rr[0] ^= 1
            return "dve" if evac_rr[0] else "act"

        def load_block(b, banks=None):
            sl = b % 2
            P.dma("sp", stage[A, sl, A], x_d[b * 128:(b + 1) * 128, :], "xs%d" % sl)
            for half in range(2):
                pb = bank(half + 2 * (b % 2)) if banks is None else bank(banks[half])
                for cc in range(4):
                    c = half * 4 + cc
                    P.tr(pb[A, cc * 128:(cc + 1) * 128], stage[A, sl, c * 128:(c + 1) * 128], identf)
                P.cp(evac_eng(), xT[A, half * 4:half * 4 + 4, b * 128:(b + 1) * 128],
                     View(pb.ap.rearrange("p (a b) -> p a b", a=4), pb.all().blocks))

        DEFER = upto >= 2
        for b in range(8 if DEFER else NT):
            load_block(b)

        for blk in range(2 if upto >= 1 else 0):
            P.dma("sp", memst.all(), mem_d[blk * 128:(blk + 1) * 128, :], "ms")
            P.act(memh.all(), memst.all(), AF.Square, accum=scr[A, 0:1])
            P.act(scr[A, 1:2], scr[A, 0:1], AF.Ln, scale=1.0 / D, bias=EPS)
            P.act(scr[A, 2:3], scr[A, 1:2], AF.Exp, scale=-0.5)
            P.ts("dve", memh.all(), memst.all(), scr[A, 2:3], None, op0=ALU.mult)
            pb = bank(7, BF16)
            for c in range(KC):
                P.tr(pb[A, c * 128:(c + 1) * 128], memh[A, c * 128:(c + 1) * 128], identb.all())
            P.cp("dve", hmT[A, A, blk * 128:(blk + 1) * 128],
                 View(pb.ap.rearrange("p (a b) -> p a b", a=KC), pb.all().blocks))

        def wload(tile_view, dram_ap, key):
            P.dma("pool", tile_view, dram_ap, key)

        def wv(name):
            return Wd[name].rearrange("(k p) f -> p k f", p=128)

        def mem_kv(layer, wmkv, hmg):
            KmT, Vm = KmTs[layer], Vms[layer]
            wload(wmkv.all(), wv(layer + "_w_mem_kv"), "wmkv_" + layer)
            for c in range(KC):
                P.ts("dve", hmg[A, c, A], hmT[A, c, A], col(layer + "_mem_norm", c), None, op0=ALU.mult)
            for cc in range(2):
                pb = bank(6)
                for k in range(KC):
                    P.mm(pb[A, 0:NMEM], wmkv[A, k, cc * 128:(cc + 1) * 128], hmg[A, k, A], start=(k == 0), stop=(k == KC - 1))
                P.cp("dve", KmT[A, cc, A], pb[A, 0:NMEM])
            P.memset("pool", Vm[A, A, A, 64:65], 1.0)
            for blk in range(2):
                pb = bank(6)
                for k in range(KC):
                    P.mm(pb[A, 0:256], hmg[A, k, blk * 128:(blk + 1) * 128], wmkv[A, k, 256:512], start=(k == 0), stop=(k == KC - 1))
                P.cp("dve", Vm[A, blk, A, 0:64], View(pb.ap[:, 0:256].rearrange("p (a b) -> p a b", a=4), pb[A, 0:256].blocks))

        if upto >= 1:
            mem_kv("b", b_wmkv, b_hmg)
            mem_kv("a", a_wmkv, a_hmg)

        def norm_tile(name, t0, slot, rs):
            pb = bank(7)
            for c in range(KC):
                P.act(sq[A, c % 2, A], xT[A, c, t0:t0 + 512], AF.Square)
                P.mm(pb.all(), onesb.all(), sq[A, c % 2, A], start=(c == 0), stop=(c == KC - 1))
            P.act(rstd[A, rs, A], pb.all(), AF.Ln, scale=1.0 / D, bias=EPS)
            P.act(rstd[A, rs, A], rstd[A, rs, A], AF.Exp, scale=-0.5)
            for c in range(KC):
                P.stt(hT[A, slot, c, A], xT[A, c, t0:t0 + 512], col(name, c), rstd[A, rs, A], ALU.mult, ALU.mult)

        att_state = dict(n=0)
        vp_state = [0]
        NVP = 16

        def attention(groups, sring=(0, 1), L=1, after_qk=None):
            chunks = []
            for tiles in groups:
                chunks += [tiles[i:i + 4] for i in range(0, len(tiles), 4)]
            info = []

            def qk(ci):
                n = att_state["n"]
                att_state["n"] += 1
                sb_ = bank(sring[n % len(sring)])
                slot = n % 4
                info.append((sb_, slot))
                for jj, t in enumerate(chunks[ci]):
                    o_ = sb_[A, jj * 128:(jj + 1) * 128]
                    P.mm(o_, t["lhsT"], t["rhs"], start=True, stop=not t["mask"])
                    if t["mask"]:
                        P.mm(o_, identb.all(), maskb.all(), start=False, stop=True)

            def ex_pv(ci):
                sb_, slot = info[ci]
                n = len(chunks[ci])
                P.act(ptr_[A, slot, 0:n * 128], sb_[A, 0:n * 128], AF.Exp, scale=0.125)
                for jj, t in enumerate(chunks[ci]):
                    if t.get("pre") is not None:
                        t["pre"]()
                    rhs_ = t["pv_rhs"]
                    if t.get("vs") is not None:
                        vslot = vp_state[0] % NVP
                        vp_state[0] += 1
                        eng_ = "dve"
                        if eng_ == "pool":
                            P.ts("pool", vprime[A, vslot, A], rhs_, t["vs"], 1.0, op0=ALU.mult, op1=ALU.mult)
                        else:
                            P.ts("dve", vprime[A, vslot, A], rhs_, t["vs"], None, op0=ALU.mult)
                        rhs_ = vprime[A, vslot, A]
                    P.mm(t["pv_out"], ptr_[A, slot, jj * 128:(jj + 1) * 128], rhs_, start=t["start"], stop=t["stop"])

            for ci in range(min(L, len(chunks))):
                qk(ci)
            if after_qk is not None:
                after_qk()
            for ci in range(len(chunks)):
                if ci + L < len(chunks):
                    qk(ci + L)
                ex_pv(ci)

        def mem_tiles(layer, qmT, qi, ombank, qz=None):
            KmT, Vm = KmTs[layer], Vms[layer]
            tls = [[], []]
            for h in (0, 2, 1, 3):
                cc, po = h // 2, (h % 2) * 64
                tl = tls[h % 2] if qz is None else tls[0]
                for kb in range(2):
                    if qz is None:
                        l_, r_ = KmT[po:po + 64, cc, kb * 128:(kb + 1) * 128], qmT[po:po + 64, cc, qi * 128:(qi + 1) * 128]
                    else:
                        l_, r_ = KmT[A, cc, kb * 128:(kb + 1) * 128], qz[A, cc, h % 2, A]
                    tl.append(dict(lhsT=l_, rhs=r_,
                                   mask=False, vs=None, pv_rhs=Vm[A, kb, h, A], pv_out=ombank[A, h * 65:(h + 1) * 65],
                                   start=(kb == 0), stop=(kb == 1)))
            return tls

        def mem_finish(ombank, tokblk=tokblk):
            ov = View(ombank.ap[:, 0:260].rearrange("p (a b) -> p a b", a=4), ombank[A, 0:260].blocks)
            P.recip(orec[A, 12:16], View(ov.ap[:, :, 64], ov.blocks))
            P.tt("dve", View(tokblk.ap[:, 768:1024].rearrange("p (a b) -> p a b", a=4), tokblk[A, 768:1024].blocks),
                 View(ov.ap[:, :, 0:64], ov.blocks), orec[A, 12:16], ALU.mult,
                 in1_ap=orec.ap[:, 12:16].unsqueeze(2).broadcast_to([128, 4, 64]))

        def cat_block(qi, tokblk=tokblk):
            pb = bank(7, BF16)
            for c in range(KC):
                P.tr(pb[A, c * 128:(c + 1) * 128], tokblk[A, c * 128:(c + 1) * 128], identb.all())
            P.cp("dve", cat[A, A, qi * 128:(qi + 1) * 128], View(pb.ap.rearrange("p (a b) -> p a b", a=KC), pb.all().blocks))

        def out_proj(t0):
            for dc in range(KC):
                pb = bank(6 + (dc % 2))
                for k in range(KC):
                    P.mm(pb.all(), wout[A, k, dc * 128:(dc + 1) * 128], cat[A, k, A], start=(k == 0), stop=(k == KC - 1))
                P.tt("dve", xT[A, dc, t0:t0 + 512], pb.all(), xT[A, dc, t0:t0 + 512], ALU.add)

        def ffn(layer):
            wi = wv(layer + "_ffn_in")
            wo = Wd[layer + "_ffn_out"].rearrange("(c p) d -> p c d", p=128)
            nq = [0]
            ns = [0]

            def load_quad(q):
                sl = nq[0] % 2
                nq[0] += 1
                wload(fwin[A, sl, A, 0:256], wi[:, :, q * 256:(q + 1) * 256], "fwg%d" % sl)
                wload(fwin[A, sl, A, 256:512], wi[:, :, DFF + q * 256:DFF + (q + 1) * 256], "fwu%d" % sl)
                return sl

            def load_slab(dc):
                sl = ns[0] % 2
                ns[0] += 1
                wload(fwout[A, sl, A, A], wo[:, :, dc * 128:(dc + 1) * 128], "fwo%d" % sl)
                return sl

            nchunk = [0]
            for tt_ in range(2):
                norm_tile(layer + "_norm2", tt_ * 512, tt_, tt_)
            for half in range(2):
                h0 = half * 1024
                pend = load_quad(0)
                for q in range(11):
                    sl = pend
                    if q + 1 < 11:
                        pend = load_quad(q + 1)
                    for sub in range(2):
                        c = 2 * q + sub
                        buf = c % 2
                        for kind in range(2):
                            ch = c + kind * NCH
                            pj = nchunk[0] % 3
                            nchunk[0] += 1
                            pb2 = bank2(pj)
                            wc = kind * 256 + sub * 128
                            for k in range(KC):
                                for tb in range(2):
                                    P.mm(pb2[A, tb * 512:(tb + 1) * 512], fwin[A, sl, k, wc:wc + 128], hT[A, tb, k, A],
                                         start=(k == 0), stop=(k == KC - 1))
                            h_ = hc[A, buf, kind, A]
                            P.act(h_, pb2.all(), AF.Identity, bias=col(layer + "_cb", ch), scale=col(layer + "_cw2", ch))
                            P.stt(hc[A, buf, kind, 1:1024], pb2[A, 0:1023], col(layer + "_cw1", ch), hc[A, buf, kind, 1:1024], ALU.mult, ALU.add)
                            P.stt(hc[A, buf, kind, 2:1024], pb2[A, 0:1022], col(layer + "_cw0", ch), hc[A, buf, kind, 2:1024], ALU.mult, ALU.add)
                            if half == 1:
                                P.stt(hc[A, buf, kind, 0:1], halo[A, ch, 1:2], col(layer + "_cw1", ch), hc[A, buf, kind, 0:1], ALU.mult, ALU.add)
                                P.stt(hc[A, buf, kind, 0:2], halo[A, ch, 0:2], col(layer + "_cw0", ch), hc[A, buf, kind, 0:2], ALU.mult, ALU.add)
                            else:
                                P.cp("dve", halo[A, ch, A], pb2[A, 1022:1024])
                        P.act(hc[A, buf, 0, A], hc[A, buf, 0, A], AF.Silu)
                        P.tt("pool", gT[A, c, A], hc[A, buf, 0, A], hc[A, buf, 1, A], ALU.mult)
                if half == 0:
                    for tt_ in range(2):
                        norm_tile(layer + "_norm2", 1024 + tt_ * 512, tt_, tt_)
                pend = load_slab(0)
                for dc in range(KC):
                    sl = pend
                    if dc + 1 < KC:
                        pend = load_slab(dc + 1)
                    for tb in range(2):
                        pb = bank(6 + tb)
                        for c in range(NCH):
                            P.mm(pb.all(), fwout[A, sl, c, A], gT[A, c, tb * 512:(tb + 1) * 512], start=(c == 0), stop=(c == NCH - 1))
                        t0 = h0 + tb * 512
                        P.tt("dve", xT[A, dc, t0:t0 + 512], pb.all(), xT[A, dc, t0:t0 + 512], ALU.add)

        def layer_a():
            wload(a_win[A, A, 0:896], wv("a_w_in")[:, :, 0:896], "win0")
            wload(a_win[A, A, 896:1792], wv("a_w_in")[:, :, 896:1792], "win1")
            wload(wout.all(), wv("a_w_out"), "wout")
            P.dma("sp", vnbc.all(), vnbc_d, "c3")
            P.dma("sp", wstage.all(), ws_d.rearrange("g t s -> t g s"), "c4")
            P.tt("dve", wmask.all(), wstage.all(), consts[A, 1, A], ALU.mult,
                 in1_ap=consts.ap[:, 1:2, :].broadcast_to([128, 4, 128]))
            pb = bank(7, BF16)
            for g in range(4):
                P.tr(pb[A, g * 128:(g + 1) * 128], wmask[A, g, A], identb.all())
            P.cp("dve", wsT.all(), View(pb.ap[:, 0:512].rearrange("p (a b) -> p a b", a=4), pb[A, 0:512].blocks))

            def prep_tile(t):
                slot = t % 2
                norm_tile("a_norm1", t * 512, slot, slot)
                for cc in range(2):
                    pb = bank(6)
                    for k in range(KC):
                        P.mm(pb.all(), a_win[A, k, 1536 + cc * 128:1536 + (cc + 1) * 128], hT[A, slot, k, A], start=(k == 0), stop=(k == KC - 1))
                    P.cp("act", a_qmT2[slot][A, cc, A], pb.all())

            def zmm(b, ks):
                t, qi = b // 4, b % 4
                slot = t % 2
                zb = [bank(2), bank(3), bank(4)]
                for k in ks:
                    for n in range(3):
                        P.mm(zb[n].all(), hT[A, slot, k, qi * 128:(qi + 1) * 128], a_win[A, k, n * 512:(n + 1) * 512],
                             start=(k == 0), stop=(k == KC - 1))

            def zphase(b):
                zmm(b, range(KC))
                zact(b)

            def zact(b):
                u_, v_, vn_ = u_t2[b % 2], v_t2[b % 2], vn_t2[b % 2]
                zb = [bank(2), bank(3), bank(4)]
                P.act(u_[A, 0:512], zb[0].all(), AF.Gelu_apprx_tanh)
                P.act(u_[A, 512:768], zb[1][A, 0:256], AF.Gelu_apprx_tanh)
                P.act(v_[A, 0:256], zb[1][A, 256:512], AF.Gelu_apprx_tanh)
                P.act(v_[A, 256:768], zb[2].all(), AF.Gelu_apprx_tanh)
                P.act(vn_.all(), v_.all(), AF.Square, accum=scr[A, 4 + 4 * (b % 2):5 + 4 * (b % 2)])
                P.act(scr[A, 5 + 4 * (b % 2):6 + 4 * (b % 2)], scr[A, 4 + 4 * (b % 2):5 + 4 * (b % 2)], AF.Ln, scale=1.0 / TOK, bias=EPS)
                P.act(scr[A, 6 + 4 * (b % 2):7 + 4 * (b % 2)], scr[A, 5 + 4 * (b % 2):6 + 4 * (b % 2)], AF.Exp, scale=-0.5)
                P.stt(vn_.all(), v_.all(), scr[A, 6 + 4 * (b % 2):7 + 4 * (b % 2)], vnbc.all(), ALU.mult, ALU.mult)

            def m1phase(b):
                u_, vn_ = u_t2[b % 2], vn_t2[b % 2]
                tk = tokb2[b % 2]
                mb = bank(5)
                for gp in range(2):
                    for g2 in range(2):
                        g = gp * 2 + g2
                        P.mm(mb[A, g2 * 192:(g2 + 1) * 192], wsT[A, g, A], vn_[A, g * 192:(g + 1) * 192])
                    for g2 in range(2):
                        g = gp * 2 + g2
                        P.stt(tk[A, g * 192:(g + 1) * 192], mb[A, g2 * 192:(g2 + 1) * 192], col("a_b_s", g),
                              u_[A, g * 192:(g + 1) * 192], ALU.add, ALU.mult)

            def m2phase(b, zb_=None):
                t, qi = b // 4, b % 4
                tk = tokb2[b % 2]
                omb = bank(6)
                hook = (lambda: zmm(zb_, range(0, 4))) if zb_ is not None else None
                attention(mem_tiles('a', a_qmT2[t % 2], qi, omb), L=2, after_qk=hook)
                if zb_ is not None:
                    zmm(zb_, range(4, KC))
                mem_finish(omb, tk)
                cat_block(qi, tk)

            prep_tile(0)
            zphase(0)
            zphase(1)
            m1phase(0)
            for b in range(NT):
                if b < 8:
                    load_block(8 + b, banks=(7, 5))
                if b % 4 == 1 and b // 4 + 1 < 4:
                    prep_tile(b // 4 + 1)
                m2phase(b, b + 2 if b + 2 < NT else None)
                if b + 1 < NT:
                    m1phase(b + 1)
                if b + 2 < NT:
                    zact(b + 2)
                if b % 4 == 3:
                    out_proj((b // 4) * 512)

        def kv_phase():
            wload(wkv[A, A, 0:768], wv("w_kv")[:, :, 0:768], "win0")
            wload(wkv[A, A, 768:1548], wv("w_kv")[:, :, 768:1548], "win1")
            P.memset("pool", Vaug[A, A, A, 64:65], 1.0)
            norm_tile("kv_norm", 0, 0, 0)
            for t in range(4):
                t0 = t * 512
                slot = t % 2
                for hp in range(6):
                    pb = bank(hp % 2)
                    for k in range(KC):
                        P.mm(pb.all(), wkv[A, k, hp * 128:(hp + 1) * 128], hT[A, slot, k, A], start=(k == 0), stop=(k == KC - 1))
                    P.cp(evac_eng(), KT[A, hp, t0:t0 + 512], pb.all())
                if t + 1 < 4:
                    norm_tile("kv_norm", t0 + 512, (t + 1) % 2, (t + 1) % 2)
                for qi in range(4):
                    blk = t * 4 + qi
                    pa, pbk = bank(2 + 2 * (qi % 2)), bank(3 + 2 * (qi % 2))
                    for k in range(KC):
                        P.mm(pa[A, 0:384], hT[A, slot, k, qi * 128:(qi + 1) * 128], wkv[A, k, 768:1152], start=(k == 0), stop=(k == KC - 1))
                        P.mm(pbk[A, 0:396], hT[A, slot, k, qi * 128:(qi + 1) * 128], wkv[A, k, 1152:1548], start=(k == 0), stop=(k == KC - 1))
                    P.cp("act", Vaug[A, blk, 0:6, 0:64], View(pa.ap[:, 0:384].rearrange("p (a b) -> p a b", a=6), pa[A, 0:384].blocks))
                    P.cp("dve", Vaug[A, blk, 6:12, 0:64], View(pbk.ap[:, 0:384].rearrange("p (a b) -> p a b", a=6), pbk[A, 0:384].blocks))
                    P.tt("dve", scr[A, 16:28], pbk[A, 384:396], bfbc[A, 0:12], ALU.add)
                    P.act(scr[A, 32:44], scr[A, 16:28], AF.Exp, scale=-1.0)
                    P.act(scr[A, 48:60], scr[A, 32:44], AF.Ln, bias=1.0)
                    P.ts("dve", lf[A, blk, A], scr[A, 48:60], -1.0, None, op0=ALU.mult)
            P.memset("dve", lfx[A, 0, A], 0.0)
            for b in range(1, NT):
                P.tt("dve", lfx[A, b, A], lfx[A, b - 1, A], lf[A, b - 1, A], ALU.add)
            lf2 = View(lf.ap.rearrange("p a b -> p (a b)"), lf.all().blocks)
            lfx2 = View(lfx.ap.rearrange("p a b -> p (a b)"), lfx.all().blocks)
            pb = bank(6)
            P.mm(pb[A, 0:192], consts[A, 2, A], lf2, start=True, stop=False)
            P.mm(pb[A, 0:192], consts[A, 4, A], lfx2, start=False, stop=True)
            P.cp("dve", View(cfox.ap.rearrange("p a b -> p (a b)"), cfox.all().blocks), pb[A, 0:192])
            pb = bank(7)
            P.mm(pb[A, 0:192], consts[A, 3, A], lf2, start=True, stop=False)
            P.mm(pb[A, 0:192], consts[A, 4, A], lfx2, start=False, stop=True)
            P.cp("dve", View(rbc.ap.rearrange("p a b -> p (a b)"), rbc.all().blocks), pb[A, 0:192])

        def layer_b():
            wload(b_wq.all(), wv("b_w_q"), "win0")
            wload(wout.all(), wv("b_w_out"), "wout")
            P.memset("pool", QTz_all.all(), 0.0)
            def qproj():
                for hp in range(8):
                    pb = bank(6 + hp % 2)
                    for k in range(KC):
                        P.mm(pb.all(), b_wq[A, k, hp * 128:(hp + 1) * 128], hT[A, 0, k, A], start=(k == 0), stop=(k == KC - 1))
                    dst = b_QT[A, hp, A] if hp < 6 else b_qmT[A, hp - 6, A]
                    P.cp("dve", dst, pb.all())

            norm_tile("b_norm1", 0, 0, 0)
            qproj()

            def prologue(i):
                t, qi = i // 4, i % 4
                bi = i % 2
                P.tt("dve", bias_i[A, bi, 0:i + 1, A], rbc[A, i:i + 1, A], cfox[A, 0:i + 1, A], ALU.subtract,
                     in0_ap=rbc.ap[:, i:i + 1, :].broadcast_to([128, i + 1, 12]))
                P.act(eb_bf[A, bi, 0:i + 1, A], bias_i[A, bi, 0:i + 1, A], AF.Exp)
                for par in range(2):
                    rows = slice(par * 64, par * 64 + 64)
                    P.cp("dve", QTzf[bi][rows, A, par, A], b_QT[rows, A, qi * 128:(qi + 1) * 128])
                    P.cp("dve", QTzm[bi][rows, A, par, A], b_qmT[rows, A, qi * 128:(qi + 1) * 128])
                tl = []
                vscales = []
                obanks = [bank(2), bank(3)]
                for h in (0, 2, 4, 6, 8, 10, 1, 3, 5, 7, 9, 11):
                    hp = h // 2
                    ob = obanks[h % 2]
                    hc_ = h // 2
                    vb = vp[vp_state[0] % 2]
                    vp_state[0] += 1

                    def vscale(vb=vb, h=h):
                        P.tt("dve", vb[A, 0:i + 1, A], Vaug[A, 0:i + 1, h, A], eb_bf[A, bi, 0:i + 1, h], ALU.mult,
                             in1_ap=eb_bf.ap[:, bi, 0:i + 1, h].unsqueeze(2).broadcast_to([128, i + 1, 65]))
                    vscales.append(vscale)
                    for j in range(i + 1):
                        tl.append(dict(lhsT=KT[A, hp, j * 128:(j + 1) * 128], rhs=QTzf[bi][A, hp, h % 2, A],
                                       mask=(j == i), vs=None, pv_rhs=vb[A, j, A],
                                       pv_out=ob[A, hc_ * 65:(hc_ + 1) * 65], start=(j == 0), stop=(j == i)))
                vscales[0]()
                for n in range(12):
                    if n + 1 < 12:
                        tl[n * (i + 1)]["pre"] = vscales[n + 1]
                omb = bank(6)
                tl = tl + mem_tiles('b', b_qmT, qi, omb, qz=QTzm[bi])[0]
                return tl, obanks, omb

            def epilogue(i, obanks, omb):
                qi = i % 4
                for par in range(2):
                    ob = obanks[par]
                    ov = View(ob.ap[:, 0:390].rearrange("p (a b) -> p a b", a=6), ob[A, 0:390].blocks)
                    P.recip(orec[A, par * 6:par * 6 + 6], View(ov.ap[:, :, 64], ov.blocks))
                    tv = tokblk.ap[:, 0:768].rearrange("p (a b c) -> p a b c", a=6, b=2)[:, :, par, :]
                    P.tt("dve", View(tv, tokblk[A, 0:768].blocks), View(ov.ap[:, :, 0:64], ov.blocks), orec[A, par * 6:par * 6 + 6], ALU.mult,
                         in1_ap=orec.ap[:, par * 6:par * 6 + 6].unsqueeze(2).broadcast_to([128, 6, 64]))
                mem_finish(omb)
                cat_block(qi)

            nxt = prologue(0)
            for i in range(NT):
                t, qi = i // 4, i % 4
                tl, obanks, omb = nxt
                attention([tl], sring=(0, 1, 4, 5), L=2)
                if qi == 0 and t + 1 < 4:
                    norm_tile("b_norm1", (t + 1) * 512, 0, (t + 1) % 2)
                if i + 1 < NT:
                    nxt = prologue(i + 1)
                epilogue(i, obanks, omb)
                if qi == 2 and t + 1 < 4:
                    qproj()
                if qi == 3:
                    out_proj(t * 512)

        if upto >= 2:
            layer_a()
        if upto >= 3:
            ffn("a")
        if upto >= 4:
            kv_phase()
        if upto >= 5:
            layer_b()
        if upto >= 6:
            ffn("b")

        P.force = True
        for t in range(4):
            t0 = t * 512
            pb = bank(7)
            for c in range(KC):
                P.act(sq[A, c % 2, A], xT[A, c, t0:t0 + 512], AF.Square)
                P.mm(pb.all(), onesb.all(), sq[A, c % 2, A], start=(c == 0), stop=(c == KC - 1))
            P.act(rstd[A, 0, A], pb.all(), AF.Ln, scale=1.0 / D, bias=EPS)
            P.act(rstd[A, 0, A], rstd[A, 0, A], AF.Exp, scale=-0.5)
            for c in range(KC):
                P.stt(xT[A, c, t0:t0 + 512], xT[A, c, t0:t0 + 512], col("final_norm", c), rstd[A, 0, A], ALU.mult, ALU.mult)
            for qi in range(4):
                b = t * 4 + qi
                sl = b % 2
                for half in range(2):
                    pbk = bank(half + 2 * sl)
                    for cc in range(4):
                        c = half * 4 + cc
                        P.tr(pbk[A, cc * 128:(cc + 1) * 128], xT[A, c, b * 128:(b + 1) * 128], identf)
                    P.cp(evac_eng(), stage[A, sl, half * 512:(half + 1) * 512], pbk.all())
                P.dma("sp", out_d[b * 128:(b + 1) * 128, :], stage[A, sl, A], "os%d" % sl)

        P.emit(final_wait_keys=["os0", "os1"])
    return nc


def _host_consts():
    c = np.zeros((128, 6, 128), np.float32)
    i = np.arange(128)
    c[:, 0, :] = np.eye(128, dtype=np.float32)
    c[:, 1, :] = (i[:, None] >= i[None, :])
    c[:, 2, :] = (i[:, None] <= i[None, :])
    c[:, 3, :] = (i[:, None] <= 64)
    c[:, 4, :] = 1.0
    c[:, 5, :] = np.where(i[:, None] > i[None, :], -30000.0, 0.0)
    return c


_NC_CACHE = {}


def kernel(**inp):
    f32 = lambda a: np.ascontiguousarray(np.asarray(a, dtype=np.float32))
    cols = np.zeros((128, NCOL), np.float32)

    def put(name, vec):
        v = f32(vec).reshape(-1, 128).T
        cols[:, COL[name]:COL[name] + v.shape[1]] = v

    put("a_norm1", inp["a_norm1"][0]); put("a_norm2", inp["a_norm2"][0]); put("kv_norm", inp["kv_norm"])
    put("b_norm1", inp["b_norm1"][0]); put("b_norm2", inp["b_norm2"][0]); put("final_norm", inp["final_norm"])
    put("a_mem_norm", inp["a_mem_norm"][0]); put("b_mem_norm", inp["b_mem_norm"][0])
    for l in ("a", "b"):
        cw = f32(inp[l + "_ffn_conv"])[0]
        for j in range(3):
            put("%s_cw%d" % (l, j), cw[j])
        put(l + "_cb", inp[l + "_ffn_conv_b"][0])
    put("a_b_s", f32(inp["a_b_s"])[0].reshape(-1))
    shared = {
        "cols": cols, "consts": _host_consts(),
        "vnbc": np.ascontiguousarray(np.broadcast_to(f32(inp["a_v_norm"])[0][None, :], (128, TOK))),
        "bfbc": np.ascontiguousarray(np.broadcast_to(np.pad(f32(inp["b_f"]), (0, 4))[None, :], (128, 16))),
        "a_w_s": f32(inp["a_w_s"])[0],
    }
    for n in WEIGHTS:
        a = f32(inp[n])
        shared[n] = a[0] if a.ndim == 3 else a
    x = f32(inp["x"])
    mem = f32(inp["mem"])
    if "nc" not in _NC_CACHE:
        _NC_CACHE["nc"] = build_program()
    nc = _NC_CACHE["nc"]
    in_maps = []
    for c in range(NCORES):
        m = dict(shared)
        m["x"] = x[c]
        m["mem"] = mem[c]
        in_maps.append(m)
    res = run_bass_kernel_spmd(nc, in_maps, core_ids=list(range(NCORES)))
    return np.stack([np.asarray(r["out"], dtype=np.float32) for r in res.results], axis=0)
```

```python
import contextlib
import os
import numpy as np
import concourse.bass as bass
import concourse.mybir as mybir
from concourse.bass_utils import run_bass_kernel_spmd

F32 = mybir.dt.float32
BF16 = mybir.dt.bfloat16
AF = mybir.ActivationFunctionType
ALU = mybir.AluOpType

S = 2048
D = 1024
NT = 16
KC = 8
NMEM = 256
TOK = 768
DFF = 2816
NCH = 22
EPS = 1e-6
NCORES = 8

SB_BLK = 64
PS_BLK = 2048
PS_OFF = 1 << 14


class View:
    __slots__ = ("ap", "blocks")

    def __init__(self, ap, blocks):
        self.ap = ap
        self.blocks = blocks


class Tile:
    def __init__(self, ap, space, base, shape, esize):
        self.ap = ap
        self.space = space
        self.shape = tuple(shape)
        self.esize = esize
        n = int(np.prod(shape))
        self.offs = (base + np.arange(n, dtype=np.int64) * esize).reshape(shape)

    def __getitem__(self, idx):
        if not isinstance(idx, tuple):
            idx = (idx,)
        f = idx[1:]
        sub = self.offs[f] if f else self.offs
        sub = np.asarray(sub).ravel()
        blk = SB_BLK if self.space == "sb" else PS_BLK
        lo = sub // blk
        hi = (sub + self.esize - 1) // blk
        b = np.unique(np.concatenate([lo, hi]))
        if self.space == "ps":
            b = b + PS_OFF
        return View(self.ap[idx], b)

    def all(self):
        return self[(slice(None),)]


class Prog:
    ENGS = ("pe", "act", "dve", "pool", "sp")

    def __init__(self, nc):
        self.nc = nc
        self.ops = []
        nb = PS_OFF + 512
        self.last_w = np.full(nb, -1, dtype=np.int64)
        self.last_r = {}
        self.nb = nb
        self.cnt = {e: 0 for e in self.ENGS}
        self.dcnt = {}
        self.limit = int(os.environ.get('KLIMIT', '0')) or None
        self.force = False

    def _cls(self, op):
        return ("dma", op["dma"]) if op["dma"] is not None else ("eng", op["eng"])

    def op(self, eng, emit, reads=(), writes=(), dma=None):
        i = len(self.ops)
        if self.limit is not None and i >= self.limit and not self.force:
            return
        o = dict(eng=eng, emit=emit, dma=dma, tok=None, need=False)
        cls = self._cls(o)
        rb = [v.blocks for v in reads if isinstance(v, View)]
        wb = [v.blocks for v in writes if isinstance(v, View)]
        R = np.unique(np.concatenate(rb)) if rb else np.zeros(0, dtype=np.int64)
        Wb = np.unique(np.concatenate(wb)) if wb else np.zeros(0, dtype=np.int64)
        if R.size and (R >= PS_OFF).any():
            Wb = np.unique(np.concatenate([Wb, R[R >= PS_OFF]]))
        deps = {}

        def consider(arr, kind):
            for p in np.unique(arr):
                p = int(p)
                if p < 0 or p == i:
                    continue
                po = self.ops[p]
                pc = self._cls(po)
                if pc == cls and cls[0] == "eng":
                    if eng == "pe":
                        continue
                    if kind != "raw":
                        continue
                if pc not in deps or deps[pc] < p:
                    deps[pc] = p

        if R.size:
            consider(self.last_w[R], "raw")
        if Wb.size:
            consider(self.last_w[Wb], "waw")
            for c, arr in self.last_r.items():
                consider(arr[Wb], "war")
        if R.size:
            if cls not in self.last_r:
                self.last_r[cls] = np.full(self.nb, -1, dtype=np.int64)
            self.last_r[cls][R] = i
        if Wb.size:
            self.last_w[Wb] = i
            for c, arr in self.last_r.items():
                arr[Wb] = -1
        o["deps"] = sorted(deps.values())
        for p in o["deps"]:
            self.ops[p]["need"] = True
        if dma is not None:
            self.dcnt[dma] = self.dcnt.get(dma, 0) + 1
            o["dn"] = self.dcnt[dma]
        self.ops.append(o)

    def emit(self, final_wait_keys=()):
        nc = self.nc
        ops = self.ops
        cnt = {e: 0 for e in self.ENGS}
        for o in ops:
            if o["dma"] is not None:
                o["tok"] = (("dma", o["dma"]), 16 * o["dn"])
            elif o["need"]:
                cnt[o["eng"]] += 1
                o["tok"] = (("eng", o["eng"]), cnt[o["eng"]])
        dma_keys = sorted(self.dcnt)
        with contextlib.ExitStack() as st:
            sems = {}
            for e in self.ENGS:
                sems[("eng", e)] = st.enter_context(nc.semaphore("s_" + e))
            for k in dma_keys:
                sems[("dma", k)] = st.enter_context(nc.semaphore("d_" + k))
            block = st.enter_context(nc.Block())

            def run(engname, eng):
                waited = {}
                for o in ops:
                    if o["eng"] != engname:
                        continue
                    for p in o["deps"]:
                        skey, val = ops[p]["tok"]
                        if waited.get(skey, 0) >= val:
                            continue
                        eng.wait_ge(sems[skey], val)
                        waited[skey] = val
                    ins = o["emit"](eng)
                    if o["tok"] is not None:
                        skey, val = o["tok"]
                        ins.then_inc(sems[skey], 16 if skey[0] == "dma" else 1)
                if engname == "sp":
                    for k in final_wait_keys:
                        eng.wait_ge(sems[("dma", k)], 16 * self.dcnt[k])

            @block.tensor
            def _(e):
                run("pe", e)

            @block.scalar
            def _(e):
                run("act", e)

            @block.vector
            def _(e):
                run("dve", e)

            @block.gpsimd
            def _(e):
                run("pool", e)

            @block.sync
            def _(e):
                run("sp", e)

    def mm(self, out, lhsT, rhs, start=True, stop=True):
        self.op("pe", lambda e: e.matmul(out.ap, lhsT=lhsT.ap, rhs=rhs.ap, start=start, stop=stop),
                [lhsT, rhs], [out])

    def tr(self, out, in_, ident):
        self.op("pe", lambda e: e.transpose(out.ap, in_.ap, ident.ap), [in_, ident], [out])

    def act(self, out, in_, func, bias=None, scale=None, accum=None):
        kw = {}
        reads = [in_]
        if bias is not None:
            kw["bias"] = bias.ap if isinstance(bias, View) else bias
            reads.append(bias)
        if scale is not None:
            kw["scale"] = scale.ap if isinstance(scale, View) else scale
            reads.append(scale)
        writes = [out]
        if accum is not None:
            kw["accum_out"] = accum.ap
            writes.append(accum)
        self.op("act", lambda e: e.activation(out=out.ap, in_=in_.ap, func=func, **kw), reads, writes)

    def ts(self, eng, out, in0, s1, s2=None, op0=ALU.mult, op1=None):
        a1 = s1.ap if isinstance(s1, View) else s1
        a2 = s2.ap if isinstance(s2, View) else s2
        kw = {}
        if op1 is not None:
            kw["op1"] = op1
        self.op(eng, lambda e: e.tensor_scalar(out=out.ap, in0=in0.ap, scalar1=a1, scalar2=a2, op0=op0, **kw),
                [in0, s1, s2], [out])

    def stt(self, out, in0, scalar, in1, op0, op1):
        a = scalar.ap if isinstance(scalar, View) else scalar
        self.op("dve", lambda e: e.scalar_tensor_tensor(out=out.ap, in0=in0.ap, scalar=a, in1=in1.ap, op0=op0, op1=op1),
                [in0, scalar, in1], [out])

    def tt(self, eng, out, in0, in1, op, in1_ap=None, in0_ap=None):
        b = in1_ap if in1_ap is not None else in1.ap
        a = in0_ap if in0_ap is not None else in0.ap
        self.op(eng, lambda e: e.tensor_tensor(out=out.ap, in0=a, in1=b, op=op), [in0, in1], [out])

    def cp(self, eng, out, in_):
        if eng == "act":
            self.op("act", lambda e: e.copy(out=out.ap, in_=in_.ap), [in_], [out])
        else:
            self.op(eng, lambda e: e.tensor_copy(out=out.ap, in_=in_.ap), [in_], [out])

    def memset(self, eng, out, val):
        self.op(eng, lambda e: e.memset(out.ap, val), [], [out])

    def recip(self, out, in_):
        self.op("dve", lambda e: e.reciprocal(out=out.ap, in_=in_.ap), [in_], [out])

    def dma(self, q, out, in_, key):
        oa = out.ap if isinstance(out, View) else out
        ia = in_.ap if isinstance(in_, View) else in_
        self.op(q, lambda e: e.dma_start(out=oa, in_=ia), [in_], [out], dma=key)


COL = {}
_c = 0
for _n in ("a_norm1", "a_norm2", "kv_norm", "b_norm1", "b_norm2", "final_norm", "a_mem_norm", "b_mem_norm"):
    COL[_n] = _c
    _c += 8
for _l in ("a", "b"):
    for _n in ("cw0", "cw1", "cw2", "cb"):
        COL[_l + "_" + _n] = _c
        _c += 44
COL["a_b_s"] = _c
_c += 4
NCOL = _c

WEIGHTS = {
    "a_w_in": (D, 1792), "a_w_mem_kv": (D, 512), "a_w_out": (D, D), "a_ffn_in": (D, 2 * DFF), "a_ffn_out": (DFF, D),
    "w_kv": (D, 1548), "b_w_q": (D, D), "b_w_mem_kv": (D, 512), "b_w_out": (D, D), "b_ffn_in": (D, 2 * DFF),
    "b_ffn_out": (DFF, D),
}


def build_program(upto=6):
    nc = bass.Bass("TRN2", target_bir_lowering=False)

    def din(name, shape):
        return nc.dram_tensor(name, list(shape), F32, kind="ExternalInput").ap()

    x_d = din("x", (S, D))
    mem_d = din("mem", (NMEM, D))
    Wd = {n: din(n, s) for n, s in WEIGHTS.items()}
    ws_d = din("a_w_s", (4, 128, 128))
    cols_d = din("cols", (128, NCOL))
    consts_d = din("consts", (128, 6, 128))
    vnbc_d = din("vnbc", (128, TOK))
    bfbc_d = din("bfbc", (128, 16))
    out_d = nc.dram_tensor("out", [S, D], F32, kind="ExternalOutput").ap()

    st = contextlib.ExitStack()
    with st:
        ARENA = 212736
        arena = st.enter_context(nc.sbuf_tensor("arena", [128, ARENA // 2], BF16))
        pst = [st.enter_context(nc.psum_tensor("ps%d" % j, [128, 1024], F32)) for j in range(4)]

        def T(off, shape, dt):
            es = 4 if dt == F32 else 2
            n = int(np.prod(shape))
            assert off % 4 == 0 and off + n * es <= ARENA, (off, shape)
            ap = arena[:, off // 2:(off + n * es) // 2]
            if dt == F32:
                ap = ap.bitcast(F32)
            if len(shape) == 2:
                ap = ap.rearrange("p (a b) -> p a b", a=shape[0])
            elif len(shape) == 3:
                ap = ap.rearrange("p (a b c) -> p a b c", a=shape[0], b=shape[1])
            elif len(shape) == 4:
                ap = ap.rearrange("p (a b c d) -> p a b c d", a=shape[0], b=shape[1], c=shape[2])
            return Tile(ap, "sb", off, shape, es)

        def bank(b, dt=F32):
            ap = pst[b // 2][:, (b % 2) * 512:(b % 2 + 1) * 512]
            base = (b // 2) * 4096 + (b % 2) * 2048
            if dt == BF16:
                return Tile(ap.bitcast(BF16), "ps", base, (1024,), 2)
            return Tile(ap, "ps", base, (512,), 4)

        def bank2(j):
            return Tile(pst[j][:, :], "ps", j * 4096, (1024,), 4)

        o = 0
        xT = T(o, (KC, S), F32); o += 65536
        HT0 = o; hT = T(o, (2, KC, 512), BF16); o += 16384
        BIG = o; o += 49536
        WA = o; o += 28672
        WB = o; o += 16384
        CAT = o; cat = T(o, (KC, 512), BF16); o += 8192
        sm = [o]

        def small(shape, dt):
            es = 4 if dt == F32 else 2
            n = int(np.prod(shape)) * es
            n = (n + 63) // 64 * 64
            t = T(sm[0], shape, dt)
            sm[0] += n
            return t

        consts = small((6, 128), F32)
        cols = small((NCOL,), F32)
        identb = small((128,), BF16)
        maskb = small((128,), BF16)
        onesb = small((128,), BF16)
        halo = small((44, 2), F32)
        cfox = small((NT, 12), F32)
        rbc = small((NT, 12), F32)
        rstd = small((2, 512), F32)
        sq = small((2, 512), BF16)
        scr = small((64,), F32)
        bfbc = small((16,), F32)
        KmTs = {l: small((2, NMEM), BF16) for l in 'ab'}
        Vms = {l: small((2, 4, 65), BF16) for l in 'ab'}
        ptr_ = small((4, 512), BF16)
        tokblk = small((1024,), BF16)
        bias_i = small((2, NT, 12), F32)
        orec = small((16,), F32)
        vprime = small((16, 65), BF16)
        assert sm[0] <= ARENA, sm[0]

        stage = T(BIG, (2, 1024), F32)
        memst = T(BIG + 8192, (1024,), F32)
        memh = T(BIG + 12288, (1024,), BF16)
        hmT = T(BIG + 16384, (KC, NMEM), BF16)
        lf = T(CAT, (NT, 12), F32)
        lfx = T(CAT + 1024, (NT, 12), F32)
        KT = T(BIG, (6, S), BF16)
        Vaug = T(BIG + 24576, (NT, 12, 65), BF16)
        gT = T(BIG, (NCH, 1024), BF16)
        a_wmkv = T(BIG + 24576, (KC, 512), BF16)
        a_hmg = T(BIG + 32768, (KC, NMEM), BF16)
        vnbc = T(BIG + 12288, (TOK,), F32)
        u_t = T(BIG + 15360, (TOK,), F32)
        v_t = T(BIG + 18432, (TOK,), F32)
        vn_t = T(BIG + 21504, (TOK,), BF16)
        wsT = T(BIG + 23040, (4, 128), BF16)
        wstage = T(BIG + 24064, (4, 128), F32)
        wmask = T(BIG + 26112, (4, 128), BF16)
        a_qmT = T(BIG + 27136, (2, 512), BF16)
        a_qmT2 = [a_qmT, T(BIG + 29184, (2, 512), BF16)]
        u_t2 = [u_t, T(BIG + 36864, (TOK,), F32)]
        v_t2 = [v_t, T(BIG + 39936, (TOK,), F32)]
        vn_t2 = [vn_t, T(BIG + 43008, (TOK,), BF16)]
        tokb2 = [tokblk, T(BIG + 44544, (1024,), BF16)]
        a_win = T(WA, (KC, 1792), BF16)
        wkv = T(WA, (KC, 1548), BF16)
        b_wq = T(WA, (KC, D), BF16)
        b_wmkv = T(WA + 16384, (KC, 512), BF16)
        wout = T(WB, (KC, D), BF16)
        fwin = T(WA, (2, KC, 512), BF16)
        fwout = T(WA + 16384, (2, NCH, 128), BF16)
        hc = T(WB, (2, 2, 1024), F32)
        b_hmg = T(CAT, (KC, NMEM), BF16)
        QTzf = [T(WA + 16384 + z * 4096, (6, 2, 128), BF16) for z in range(2)]
        QTzm = [T(WA + 16384 + z * 4096 + 3072, (2, 2, 128), BF16) for z in range(2)]
        QTz_all = T(WA + 16384, (2 * 16 * 128,), BF16)
        vp = [vprime, T(WA + 24576, (16, 65), BF16)]
        eb_bf = T(WA + 24576 + 2112, (2, NT, 12), BF16)
        b_QT = T(HT0 + 8192, (6, 512), BF16)
        b_qmT = T(HT0 + 8192 + 6144, (2, 512), BF16)

        P = Prog(nc)
        A = slice(None)

        def col(name, j=0):
            c = COL[name] + j
            return cols[A, c:c + 1]

        P.dma("sp", consts.all(), consts_d, "c0")
        P.dma("sp", cols.all(), cols_d, "c1")
        P.dma("sp", bfbc.all(), bfbc_d, "c2")
        P.cp("dve", identb.all(), consts[A, 0, A])
        P.cp("dve", maskb.all(), consts[A, 5, A])
        P.cp("dve", onesb.all(), consts[A, 4, A])
        identf = consts[A, 0, A]

        evac_rr = [0]

        def evac_eng():
            evac_rr[0] ^= 1
            return "dve" if evac_rr[0] else "act"

        def load_block(b, banks=None):
            sl = b % 2
            P.dma("sp", stage[A, sl, A], x_d[b * 128:(b + 1) * 128, :], "xs%d" % sl)
            for half in range(2):
                pb = bank(half + 2 * (b % 2)) if banks is None else bank(banks[half])
                for cc in range(4):
                    c = half * 4 + cc
                    P.tr(pb[A, cc * 128:(cc + 1) * 128], stage[A, sl, c * 128:(c + 1) * 128], identf)
                P.cp(evac_eng(), xT[A, half * 4:half * 4 + 4, b * 128:(b + 1) * 128],
                     View(pb.ap.rearrange("p (a b) -> p a b", a=4), pb.all().blocks))

        DEFER = upto >= 2
        for b in range(8 if DEFER else NT):
            load_block(b)

        for blk in range(2 if upto >= 1 else 0):
            P.dma("sp", memst.all(), mem_d[blk * 128:(blk + 1) * 128, :], "ms")
            P.act(memh.all(), memst.all(), AF.Square, accum=scr[A, 0:1])
            P.act(scr[A, 1:2], scr[A, 0:1], AF.Ln, scale=1.0 / D, bias=EPS)
            P.act(scr[A, 2:3], scr[A, 1:2], AF.Exp, scale=-0.5)
            P.ts("dve", memh.all(), memst.all(), scr[A, 2:3], None, op0=ALU.mult)
            pb = bank(7, BF16)
            for c in range(KC):
                P.tr(pb[A, c * 128:(c + 1) * 128], memh[A, c * 128:(c + 1) * 128], identb.all())
            P.cp("dve", hmT[A, A, blk * 128:(blk + 1) * 128],
                 View(pb.ap.rearrange("p (a b) -> p a b", a=KC), pb.all().blocks))

        def wload(tile_view, dram_ap, key):
            P.dma("pool", tile_view, dram_ap, key)

        def wv(name):
            return Wd[name].rearrange("(k p) f -> p k f", p=128)

        def mem_kv(layer, wmkv, hmg):
            KmT, Vm = KmTs[layer], Vms[layer]
            wload(wmkv.all(), wv(layer + "_w_mem_kv"), "wmkv_" + layer)
            for c in range(KC):
                P.ts("dve", hmg[A, c, A], hmT[A, c, A], col(layer + "_mem_norm", c), None, op0=ALU.mult)
            for cc in range(2):
                pb = bank(6)
                for k in range(KC):
                    P.mm(pb[A, 0:NMEM], wmkv[A, k, cc * 128:(cc + 1) * 128], hmg[A, k, A], start=(k == 0), stop=(k == KC - 1))
                P.cp("dve", KmT[A, cc, A], pb[A, 0:NMEM])
            P.memset("pool", Vm[A, A, A, 64:65], 1.0)
            for blk in range(2):
                pb = bank(6)
                for k in range(KC):
                    P.mm(pb[A, 0:256], hmg[A, k, blk * 128:(blk + 1) * 128], wmkv[A, k, 256:512], start=(k == 0), stop=(k == KC - 1))
                P.cp("dve", Vm[A, blk, A, 0:64], View(pb.ap[:, 0:256].rearrange("p (a b) -> p a b", a=4), pb[A, 0:256].blocks))

        if upto >= 1:
            mem_kv("b", b_wmkv, b_hmg)
            mem_kv("a", a_wmkv, a_hmg)

        def norm_tile(name, t0, slot, rs):
            pb = bank(7)
            for c in range(KC):
                P.act(sq[A, c % 2, A], xT[A, c, t0:t0 + 512], AF.Square)
                P.mm(pb.all(), onesb.all(), sq[A, c % 2, A], start=(c == 0), stop=(c == KC - 1))
            P.act(rstd[A, rs, A], pb.all(), AF.Ln, scale=1.0 / D, bias=EPS)
            P.act(rstd[A, rs, A], rstd[A, rs, A], AF.Exp, scale=-0.5)
            for c in range(KC):
                P.stt(hT[A, slot, c, A], xT[A, c, t0:t0 + 512], col(name, c), rstd[A, rs, A], ALU.mult, ALU.mult)

        att_state = dict(n=0)
        vp_state = [0]
        NVP = 16

        def attention(groups, sring=(0, 1), L=1, after_qk=None):
            chunks = []
            for tiles in groups:
                chunks += [tiles[i:i + 4] for i in range(0, len(tiles), 4)]
            info = []

            def qk(ci):
                n = att_state["n"]
                att_state["n"] += 1
                sb_ = bank(sring[n % len(sring)])
                slot = n % 4
                info.append((sb_, slot))
                for jj, t in enumerate(chunks[ci]):
                    o_ = sb_[A, jj * 128:(jj + 1) * 128]
                    P.mm(o_, t["lhsT"], t["rhs"], start=True, stop=not t["mask"])
                    if t["mask"]:
                        P.mm(o_, identb.all(), maskb.all(), start=False, stop=True)

            def ex_pv(ci):
                sb_, slot = info[ci]
                n = len(chunks[ci])
                P.act(ptr_[A, slot, 0:n * 128], sb_[A, 0:n * 128], AF.Exp, scale=0.125)
                for jj, t in enumerate(chunks[ci]):
                    if t.get("pre") is not None:
                        t["pre"]()
                    rhs_ = t["pv_rhs"]
                    if t.get("vs") is not None:
                        vslot = vp_state[0] % NVP
                        vp_state[0] += 1
                        eng_ = "dve"
                        if eng_ == "pool":
                            P.ts("pool", vprime[A, vslot, A], rhs_, t["vs"], 1.0, op0=ALU.mult, op1=ALU.mult)
                        else:
                            P.ts("dve", vprime[A, vslot, A], rhs_, t["vs"], None, op0=ALU.mult)
                        rhs_ = vprime[A, vslot, A]
                    P.mm(t["pv_out"], ptr_[A, slot, jj * 128:(jj + 1) * 128], rhs_, start=t["start"], stop=t["stop"])

            for ci in range(min(L, len(chunks))):
                qk(ci)
            if after_qk is not None:
                after_qk()
            for ci in range(len(chunks)):
                if ci + L < len(chunks):
                    qk(ci + L)
                ex_pv(ci)

        def mem_tiles(layer, qmT, qi, ombank, qz=None):
            KmT, Vm = KmTs[layer], Vms[layer]
            tls = [[], []]
            for h in (0, 2, 1, 3):
                cc, po = h // 2, (h % 2) * 64
                tl = tls[h % 2] if qz is None else tls[0]
                for kb in range(2):
                    if qz is None:
                        l_, r_ = KmT[po:po + 64, cc, kb * 128:(kb + 1) * 128], qmT[po:po + 64, cc, qi * 128:(qi + 1) * 128]
                    else:
                        l_, r_ = KmT[A, cc, kb * 128:(kb + 1) * 128], qz[A, cc, h % 2, A]
                    tl.append(dict(lhsT=l_, rhs=r_,
                                   mask=False, vs=None, pv_rhs=Vm[A, kb, h, A], pv_out=ombank[A, h * 65:(h + 1) * 65],
                                   start=(kb == 0), stop=(kb == 1)))
            return tls

        def mem_finish(ombank, tokblk=tokblk):
            ov = View(ombank.ap[:, 0:260].rearrange("p (a b) -> p a b", a=4), ombank[A, 0:260].blocks)
            P.recip(orec[A, 12:16], View(ov.ap[:, :, 64], ov.blocks))
            P.tt("dve", View(tokblk.ap[:, 768:1024].rearrange("p (a b) -> p a b", a=4), tokblk[A, 768:1024].blocks),
                 View(ov.ap[:, :, 0:64], ov.blocks), orec[A, 12:16], ALU.mult,
                 in1_ap=orec.ap[:, 12:16].unsqueeze(2).broadcast_to([128, 4, 64]))

        def cat_block(qi, tokblk=tokblk):
            pb = bank(7, BF16)
            for c in range(KC):
                P.tr(pb[A, c * 128:(c + 1) * 128], tokblk[A, c * 128:(c + 1) * 128], identb.all())
            P.cp("dve", cat[A, A, qi * 128:(qi + 1) * 128], View(pb.ap.rearrange("p (a b) -> p a b", a=KC), pb.all().blocks))

        def out_proj(t0):
            for dc in range(KC):
                pb = bank(6 + (dc % 2))
                for k in range(KC):
                    P.mm(pb.all(), wout[A, k, dc * 128:(dc + 1) * 128], cat[A, k, A], start=(k == 0), stop=(k == KC - 1))
                P.tt("dve", xT[A, dc, t0:t0 + 512], pb.all(), xT[A, dc, t0:t0 + 512], ALU.add)

        def ffn(layer):
            wi = wv(layer + "_ffn_in")
            wo = Wd[layer + "_ffn_out"].rearrange("(c p) d -> p c d", p=128)
            nq = [0]
            ns = [0]

            def load_quad(q):
                sl = nq[0] % 2
                nq[0] += 1
                wload(fwin[A, sl, A, 0:256], wi[:, :, q * 256:(q + 1) * 256], "fwg%d" % sl)
                wload(fwin[A, sl, A, 256:512], wi[:, :, DFF + q * 256:DFF + (q + 1) * 256], "fwu%d" % sl)
                return sl

            def load_slab(dc):
                sl = ns[0] % 2
                ns[0] += 1
                wload(fwout[A, sl, A, A], wo[:, :, dc * 128:(dc + 1) * 128], "fwo%d" % sl)
                return sl

            nchunk = [0]
            for tt_ in range(2):
                norm_tile(layer + "_norm2", tt_ * 512, tt_, tt_)
            for half in range(2):
                h0 = half * 1024
                pend = load_quad(0)
                for q in range(11):
                    sl = pend
                    if q + 1 < 11:
                        pend = load_quad(q + 1)
                    for sub in range(2):
                        c = 2 * q + sub
                        buf = c % 2
                        for kind in range(2):
                            ch = c + kind * NCH
                            pj = nchunk[0] % 3
                            nchunk[0] += 1
                            pb2 = bank2(pj)
                            wc = kind * 256 + sub * 128
                            for k in range(KC):
                                for tb in range(2):
                                    P.mm(pb2[A, tb * 512:(tb + 1) * 512], fwin[A, sl, k, wc:wc + 128], hT[A, tb, k, A],
                                         start=(k == 0), stop=(k == KC - 1))
                            h_ = hc[A, buf, kind, A]
                            P.act(h_, pb2.all(), AF.Identity, bias=col(layer + "_cb", ch), scale=col(layer + "_cw2", ch))
                            P.stt(hc[A, buf, kind, 1:1024], pb2[A, 0:1023], col(layer + "_cw1", ch), hc[A, buf, kind, 1:1024], ALU.mult, ALU.add)
                            P.stt(hc[A, buf, kind, 2:1024], pb2[A, 0:1022], col(layer + "_cw0", ch), hc[A, buf, kind, 2:1024], ALU.mult, ALU.add)
                            if half == 1:
                                P.stt(hc[A, buf, kind, 0:1], halo[A, ch, 1:2], col(layer + "_cw1", ch), hc[A, buf, kind, 0:1], ALU.mult, ALU.add)
                                P.stt(hc[A, buf, kind, 0:2], halo[A, ch, 0:2], col(layer + "_cw0", ch), hc[A, buf, kind, 0:2], ALU.mult, ALU.add)
                            else:
                                P.cp("dve", halo[A, ch, A], pb2[A, 1022:1024])
                        P.act(hc[A, buf, 0, A], hc[A, buf, 0, A], AF.Silu)
                        P.tt("pool", gT[A, c, A], hc[A, buf, 0, A], hc[A, buf, 1, A], ALU.mult)
                if half == 0:
                    for tt_ in range(2):
                        norm_tile(layer + "_norm2", 1024 + tt_ * 512, tt_, tt_)
                pend = load_slab(0)
                for dc in range(KC):
                    sl = pend
                    if dc + 1 < KC:
                        pend = load_slab(dc + 1)
                    for tb in range(2):
                        pb = bank(6 + tb)
                        for c in range(NCH):
                            P.mm(pb.all(), fwout[A, sl, c, A], gT[A, c, tb * 512:(tb + 1) * 512], start=(c == 0), stop=(c == NCH - 1))
                        t0 = h0 + tb * 512
                        P.tt("dve", xT[A, dc, t0:t0 + 512], pb.all(), xT[A, dc, t0:t0 + 512], ALU.add)

        def layer_a():
            wload(a_win[A, A, 0:896], wv("a_w_in")[:, :, 0:896], "win0")
            wload(a_win[A, A, 896:1792], wv("a_w_in")[:, :, 896:1792], "win1")
            wload(wout.all(), wv("a_w_out"), "wout")
            P.dma("sp", vnbc.all(), vnbc_d, "c3")
            P.dma("sp", wstage.all(), ws_d.rearrange("g t s -> t g s"), "c4")
            P.tt("dve", wmask.all(), wstage.all(), consts[A, 1, A], ALU.mult,
                 in1_ap=consts.ap[:, 1:2, :].broadcast_to([128, 4, 128]))
            pb = bank(7, BF16)
            for g in range(4):
                P.tr(pb[A, g * 128:(g + 1) * 128], wmask[A, g, A], identb.all())
            P.cp("dve", wsT.all(), View(pb.ap[:, 0:512].rearrange("p (a b) -> p a b", a=4), pb[A, 0:512].blocks))

            def prep_tile(t):
                prep_norm(t)
                prep_qm(t)

            def prep_norm(t):
                slot = t % 2
                norm_tile("a_norm1", t * 512, slot, slot)

            def prep_qm(t):
                slot = t % 2
                for cc in range(2):
                    pb = bank(6)
                    for k in range(KC):
                        P.mm(pb.all(), a_win[A, k, 1536 + cc * 128:1536 + (cc + 1) * 128], hT[A, slot, k, A], start=(k == 0), stop=(k == KC - 1))
                    P.cp("act", a_qmT2[slot][A, cc, A], pb.all())

            def zmm(b, ks):
                t, qi = b // 4, b % 4
                slot = t % 2
                zb = [bank(2), bank(3), bank(4)]
                for k in ks:
                    for n in range(3):
                        P.mm(zb[n].all(), hT[A, slot, k, qi * 128:(qi + 1) * 128], a_win[A, k, n * 512:(n + 1) * 512],
                             start=(k == 0), stop=(k == KC - 1))

            def zphase(b):
                zmm(b, range(KC))
                zact(b)

            def zact(b):
                u_, v_, vn_ = u_t2[b % 2], v_t2[b % 2], vn_t2[b % 2]
                zb = [bank(2), bank(3), bank(4)]
                P.act(u_[A, 0:512], zb[0].all(), AF.Gelu_apprx_tanh)
                P.act(u_[A, 512:768], zb[1][A, 0:256], AF.Gelu_apprx_tanh)
                P.act(v_[A, 0:256], zb[1][A, 256:512], AF.Gelu_apprx_tanh)
                P.act(v_[A, 256:768], zb[2].all(), AF.Gelu_apprx_tanh)
                P.act(vn_.all(), v_.all(), AF.Square, accum=scr[A, 4 + 4 * (b % 2):5 + 4 * (b % 2)])
                P.act(scr[A, 5 + 4 * (b % 2):6 + 4 * (b % 2)], scr[A, 4 + 4 * (b % 2):5 + 4 * (b % 2)], AF.Ln, scale=1.0 / TOK, bias=EPS)
                P.act(scr[A, 6 + 4 * (b % 2):7 + 4 * (b % 2)], scr[A, 5 + 4 * (b % 2):6 + 4 * (b % 2)], AF.Exp, scale=-0.5)
                P.stt(vn_.all(), v_.all(), scr[A, 6 + 4 * (b % 2):7 + 4 * (b % 2)], vnbc.all(), ALU.mult, ALU.mult)

            def m1phase(b):
                u_, vn_ = u_t2[b % 2], vn_t2[b % 2]
                tk = tokb2[b % 2]
                mb = bank(5)
                for gp in range(2):
                    for g2 in range(2):
                        g = gp * 2 + g2
                        P.mm(mb[A, g2 * 192:(g2 + 1) * 192], wsT[A, g, A], vn_[A, g * 192:(g + 1) * 192])
                    for g2 in range(2):
                        g = gp * 2 + g2
                        P.stt(tk[A, g * 192:(g + 1) * 192], mb[A, g2 * 192:(g2 + 1) * 192], col("a_b_s", g),
                              u_[A, g * 192:(g + 1) * 192], ALU.add, ALU.mult)

            def m2phase(b, zb_=None):
                t, qi = b // 4, b % 4
                tk = tokb2[b % 2]
                omb = bank(6)
                hook = (lambda: zmm(zb_, range(0, 4))) if zb_ is not None else None
                attention(mem_tiles('a', a_qmT2[t % 2], qi, omb), L=2, after_qk=hook)
                if zb_ is not None:
                    zmm(zb_, range(4, KC))
                mem_finish(omb, tk)
                cat_block(qi, tk)

            prep_tile(0)
            zphase(0)
            zphase(1)
            m1phase(0)
            for b in range(NT):
                if b < 8:
                    load_block(8 + b, banks=(7, 5))
                if b % 4 == 1 and b // 4 + 1 < 4:
                    prep_norm(b // 4 + 1)
                if b % 4 == 2 and b // 4 + 1 < 4:
                    prep_qm(b // 4 + 1)
                m2phase(b, b + 2 if b + 2 < NT else None)
                if b + 1 < NT:
                    m1phase(b + 1)
                if b + 2 < NT:
                    zact(b + 2)
                if b % 4 == 3:
                    out_proj((b // 4) * 512)

        def kv_phase():
            wload(wkv[A, A, 0:768], wv("w_kv")[:, :, 0:768], "win0")
            wload(wkv[A, A, 768:1548], wv("w_kv")[:, :, 768:1548], "win1")
            P.memset("pool", Vaug[A, A, A, 64:65], 1.0)
            norm_tile("kv_norm", 0, 0, 0)
            for t in range(4):
                t0 = t * 512
                slot = t % 2
                for hp in range(6):
                    pb = bank(hp % 2)
                    for k in range(KC):
                        P.mm(pb.all(), wkv[A, k, hp * 128:(hp + 1) * 128], hT[A, slot, k, A], start=(k == 0), stop=(k == KC - 1))
                    P.cp(evac_eng(), KT[A, hp, t0:t0 + 512], pb.all())
                if t + 1 < 4:
                    norm_tile("kv_norm", t0 + 512, (t + 1) % 2, (t + 1) % 2)
                for qi in range(4):
                    blk = t * 4 + qi
                    pa, pbk = bank(2 + 2 * (qi % 2)), bank(3 + 2 * (qi % 2))
                    for k in range(KC):
                        P.mm(pa[A, 0:384], hT[A, slot, k, qi * 128:(qi + 1) * 128], wkv[A, k, 768:1152], start=(k == 0), stop=(k == KC - 1))
                        P.mm(pbk[A, 0:396], hT[A, slot, k, qi * 128:(qi + 1) * 128], wkv[A, k, 1152:1548], start=(k == 0), stop=(k == KC - 1))
                    P.cp("act", Vaug[A, blk, 0:6, 0:64], View(pa.ap[:, 0:384].rearrange("p (a b) -> p a b", a=6), pa[A, 0:384].blocks))
                    P.cp("dve", Vaug[A, blk, 6:12, 0:64], View(pbk.ap[:, 0:384].rearrange("p (a b) -> p a b", a=6), pbk[A, 0:384].blocks))
                    P.tt("dve", scr[A, 16:28], pbk[A, 384:396], bfbc[A, 0:12], ALU.add)
                    P.act(scr[A, 32:44], scr[A, 16:28], AF.Exp, scale=-1.0)
                    P.act(scr[A, 48:60], scr[A, 32:44], AF.Ln, bias=1.0)
                    P.ts("dve", lf[A, blk, A], scr[A, 48:60], -1.0, None, op0=ALU.mult)
            P.memset("dve", lfx[A, 0, A], 0.0)
            for b in range(1, NT):
                P.tt("dve", lfx[A, b, A], lfx[A, b - 1, A], lf[A, b - 1, A], ALU.add)
            lf2 = View(lf.ap.rearrange("p a b -> p (a b)"), lf.all().blocks)
            lfx2 = View(lfx.ap.rearrange("p a b -> p (a b)"), lfx.all().blocks)
            pb = bank(6)
            P.mm(pb[A, 0:192], consts[A, 2, A], lf2, start=True, stop=False)
            P.mm(pb[A, 0:192], consts[A, 4, A], lfx2, start=False, stop=True)
            P.cp("dve", View(cfox.ap.rearrange("p a b -> p (a b)"), cfox.all().blocks), pb[A, 0:192])
            pb = bank(7)
            P.mm(pb[A, 0:192], consts[A, 3, A], lf2, start=True, stop=False)
            P.mm(pb[A, 0:192], consts[A, 4, A], lfx2, start=False, stop=True)
            P.cp("dve", View(rbc.ap.rearrange("p a b -> p (a b)"), rbc.all().blocks), pb[A, 0:192])

        def layer_b():
            wload(b_wq.all(), wv("b_w_q"), "win0")
            wload(wout.all(), wv("b_w_out"), "wout")
            P.memset("pool", QTz_all.all(), 0.0)
            def qproj():
                for hp in range(8):
                    pb = bank(6 + hp % 2)
                    for k in range(KC):
                        P.mm(pb.all(), b_wq[A, k, hp * 128:(hp + 1) * 128], hT[A, 0, k, A], start=(k == 0), stop=(k == KC - 1))
                    dst = b_QT[A, hp, A] if hp < 6 else b_qmT[A, hp - 6, A]
                    P.cp(evac_eng(), dst, pb.all())

            norm_tile("b_norm1", 0, 0, 0)
            qproj()

            def prologue(i):
                t, qi = i // 4, i % 4
                bi = i % 2
                P.tt("dve", bias_i[A, bi, 0:i + 1, A], rbc[A, i:i + 1, A], cfox[A, 0:i + 1, A], ALU.subtract,
                     in0_ap=rbc.ap[:, i:i + 1, :].broadcast_to([128, i + 1, 12]))
                P.act(eb_bf[A, bi, 0:i + 1, A], bias_i[A, bi, 0:i + 1, A], AF.Exp)
                for par in range(2):
                    rows = slice(par * 64, par * 64 + 64)
                    P.cp("dve", QTzf[bi][rows, A, par, A], b_QT[rows, A, qi * 128:(qi + 1) * 128])
                    P.cp("dve", QTzm[bi][rows, A, par, A], b_qmT[rows, A, qi * 128:(qi + 1) * 128])
                tl = []
                vscales = []
                obanks = [bank(2), bank(3)]
                for h in (0, 2, 4, 6, 8, 10, 1, 3, 5, 7, 9, 11):
                    hp = h // 2
                    ob = obanks[h % 2]
                    hc_ = h // 2
                    vb = vp[vp_state[0] % 2]
                    vp_state[0] += 1

                    def vscale(vb=vb, h=h):
                        P.tt("dve", vb[A, 0:i + 1, A], Vaug[A, 0:i + 1, h, A], eb_bf[A, bi, 0:i + 1, h], ALU.mult,
                             in1_ap=eb_bf.ap[:, bi, 0:i + 1, h].unsqueeze(2).broadcast_to([128, i + 1, 65]))
                    vscales.append(vscale)
                    for j in range(i + 1):
                        tl.append(dict(lhsT=KT[A, hp, j * 128:(j + 1) * 128], rhs=QTzf[bi][A, hp, h % 2, A],
                                       mask=(j == i), vs=None, pv_rhs=vb[A, j, A],
                                       pv_out=ob[A, hc_ * 65:(hc_ + 1) * 65], start=(j == 0), stop=(j == i)))
                vscales[0]()
                for n in range(12):
                    if n + 1 < 12:
                        tl[n * (i + 1)]["pre"] = vscales[n + 1]
                omb = bank(6)
                tl = tl + mem_tiles('b', b_qmT, qi, omb, qz=QTzm[bi])[0]
                return tl, obanks, omb

            def epilogue(i, obanks, omb):
                qi = i % 4
                for par in range(2):
                    ob = obanks[par]
                    ov = View(ob.ap[:, 0:390].rearrange("p (a b) -> p a b", a=6), ob[A, 0:390].blocks)
                    P.recip(orec[A, par * 6:par * 6 + 6], View(ov.ap[:, :, 64], ov.blocks))
                    tv = tokblk.ap[:, 0:768].rearrange("p (a b c) -> p a b c", a=6, b=2)[:, :, par, :]
                    P.tt("dve", View(tv, tokblk[A, 0:768].blocks), View(ov.ap[:, :, 0:64], ov.blocks), orec[A, par * 6:par * 6 + 6], ALU.mult,
                         in1_ap=orec.ap[:, par * 6:par * 6 + 6].unsqueeze(2).broadcast_to([128, 6, 64]))
                mem_finish(omb)
                cat_block(qi)

            nxt = prologue(0)
            for i in range(NT):
                t, qi = i // 4, i % 4
                tl, obanks, omb = nxt
                attention([tl], sring=(0, 1, 4, 5), L=2)
                if qi == 0 and t + 1 < 4:
                    norm_tile("b_norm1", (t + 1) * 512, 0, (t + 1) % 2)
                if i + 1 < NT:
                    nxt = prologue(i + 1)
                epilogue(i, obanks, omb)
                if qi == 2 and t + 1 < 4:
                    qproj()
                if qi == 3:
                    out_proj(t * 512)

        if upto >= 2:
            layer_a()
        if upto >= 3:
            ffn("a")
        if upto >= 4:
            kv_phase()
        if upto >= 5:
            layer_b()
        if upto >= 6:
            ffn("b")

        P.force = True
        for t in range(4):
            t0 = t * 512
            pb = bank(7)
            for c in range(KC):
                P.act(sq[A, c % 2, A], xT[A, c, t0:t0 + 512], AF.Square)
                P.mm(pb.all(), onesb.all(), sq[A, c % 2, A], start=(c == 0), stop=(c == KC - 1))
            P.act(rstd[A, 0, A], pb.all(), AF.Ln, scale=1.0 / D, bias=EPS)
            P.act(rstd[A, 0, A], rstd[A, 0, A], AF.Exp, scale=-0.5)
            for c in range(KC):
                P.stt(xT[A, c, t0:t0 + 512], xT[A, c, t0:t0 + 512], col("final_norm", c), rstd[A, 0, A], ALU.mult, ALU.mult)
            for qi in range(4):
                b = t * 4 + qi
                sl = b % 2
                for half in range(2):
                    pbk = bank(half + 2 * sl)
                    for cc in range(4):
                        c = half * 4 + cc
                        P.tr(pbk[A, cc * 128:(cc + 1) * 128], xT[A, c, b * 128:(b + 1) * 128], identf)
                    P.cp(evac_eng(), stage[A, sl, half * 512:(half + 1) * 512], pbk.all())
                P.dma("sp", out_d[b * 128:(b + 1) * 128, :], stage[A, sl, A], "os%d" % sl)

        P.emit(final_wait_keys=["os0", "os1"])
    return nc


def _host_consts():
    c = np.zeros((128, 6, 128), np.float32)
    i = np.arange(128)
    c[:, 0, :] = np.eye(128, dtype=np.float32)
    c[:, 1, :] = (i[:, None] >= i[None, :])
    c[:, 2, :] = (i[:, None] <= i[None, :])
    c[:, 3, :] = (i[:, None] <= 64)
    c[:, 4, :] = 1.0
    c[:, 5, :] = np.where(i[:, None] > i[None, :], -30000.0, 0.0)
    return c


_NC_CACHE = {}


def kernel(**inp):
    f32 = lambda a: np.ascontiguousarray(np.asarray(a, dtype=np.float32))
    cols = np.zeros((128, NCOL), np.float32)

    def put(name, vec):
        v = f32(vec).reshape(-1, 128).T
        cols[:, COL[name]:COL[name] + v.shape[1]] = v

    put("a_norm1", inp["a_norm1"][0]); put("a_norm2", inp["a_norm2"][0]); put("kv_norm", inp["kv_norm"])
    put("b_norm1", inp["b_norm1"][0]); put("b_norm2", inp["b_norm2"][0]); put("final_norm", inp["final_norm"])
    put("a_mem_norm", inp["a_mem_norm"][0]); put("b_mem_norm", inp["b_mem_norm"][0])
    for l in ("a", "b"):
        cw = f32(inp[l + "_ffn_conv"])[0]
        for j in range(3):
            put("%s_cw%d" % (l, j), cw[j])
        put(l + "_cb", inp[l + "_ffn_conv_b"][0])
    put("a_b_s", f32(inp["a_b_s"])[0].reshape(-1))
    shared = {
        "cols": cols, "consts": _host_consts(),
        "vnbc": np.ascontiguousarray(np.broadcast_to(f32(inp["a_v_norm"])[0][None, :], (128, TOK))),
        "bfbc": np.ascontiguousarray(np.broadcast_to(np.pad(f32(inp["b_f"]), (0, 4))[None, :], (128, 16))),
        "a_w_s": f32(inp["a_w_s"])[0],
    }
    for n in WEIGHTS:
        a = f32(inp[n])
        shared[n] = a[0] if a.ndim == 3 else a
    x = f32(inp["x"])
    mem = f32(inp["mem"])
    if "nc" not in _NC_CACHE:
        _NC_CACHE["nc"] = build_program()
    nc = _NC_CACHE["nc"]
    in_maps = []
    for c in range(NCORES):
        m = dict(shared)
        m["x"] = x[c]
        m["mem"] = mem[c]
        in_maps.append(m)
    res = run_bass_kernel_spmd(nc, in_maps, core_ids=list(range(NCORES)))
    return np.stack([np.asarray(r["out"], dtype=np.float32) for r in res.results], axis=0)
```
